# Optimizing a Trainium2 kernel written in Bass

```python
import jax, jax.numpy as jnp
from jax import lax
import numpy as np

D_MODEL = 2048
BATCH = 4
SEQ = 2048
DEPTH = 4

CHUNK = 64
Q_BLOCK = 128
GLA_HEADS = 4
GLA_DK = D_MODEL // 2 // GLA_HEADS
GLA_DV = D_MODEL // GLA_HEADS
GLA_K = GLA_HEADS * GLA_DK
GLA_V = GLA_HEADS * GLA_DV
GLA_GATE_RANK = 16
GLA_GATE_TAU = 16.0
SB_HEADS = 16
SB_DH = D_MODEL // SB_HEADS
SB_W = SB_HEADS * SB_DH
D_FF = 4 * D_MODEL
EPS = 1e-6
IN_SPLITS = (GLA_K, GLA_K, GLA_V, GLA_V, GLA_GATE_RANK, SB_W, SB_W, SB_W, D_MODEL, D_MODEL)
IN_COLS = GLA_K * 2 + GLA_V * 2 + GLA_GATE_RANK + SB_W * 3 + D_MODEL * 2

kernel_name = "hybrid_gla_stickbreaking_sandwich_adaln"


def rmsnorm(x, gain):
    xf = x.astype(jnp.float32)
    y = xf * lax.rsqrt(jnp.mean(xf * xf, axis=-1, keepdims=True) + EPS)
    return (y * gain.astype(jnp.float32)).astype(x.dtype)


def gla_branch(q, k, v, r, a_low, w_gate_up, b_gate, gn_gain):
    B, S, _ = q.shape
    nc = S // CHUNK
    log_a = jax.nn.log_sigmoid((a_low @ w_gate_up + b_gate).astype(jnp.float32)) / GLA_GATE_TAU

    def heads(t, d):
        return t.astype(jnp.float32).reshape(B, nc, CHUNK, GLA_HEADS, d).transpose(1, 0, 3, 2, 4)

    qh = heads(q, GLA_DK) * (GLA_DK ** -0.5)
    kh = heads(k, GLA_DK)
    vh = heads(v, GLA_DV)
    gh = heads(log_a, GLA_DK)

    def step(state, inp):
        qc, kc, vc, gc = inp
        g = jnp.cumsum(gc, axis=2)
        g_tot = g[:, :, -1:, :]
        kv = jnp.einsum('bhck,bhcv->bhkv', kc * jnp.exp(g_tot - g), vc)
        state = jnp.exp(g_tot[:, :, 0, :, None]) * state + kv
        out = jnp.einsum('bhck,bhkv->bhcv', qc, state)
        return state, out

    s0 = jnp.zeros((B, GLA_HEADS, GLA_DK, GLA_DV), jnp.float32)
    _, o = lax.scan(step, s0, (qh, kh, vh, gh))
    o = o.transpose(1, 0, 3, 2, 4).reshape(B, S, GLA_HEADS, GLA_DV).astype(q.dtype)
    o = rmsnorm(o, gn_gain).reshape(B, S, GLA_V)
    return o * jax.nn.silu(r)


def sb_branch(q, k, v):
    B, S, _ = q.shape

    def heads(t):
        return t.reshape(B, S, SB_HEADS, SB_DH).transpose(0, 2, 1, 3)

    qh, kh, vh = heads(q), heads(k), heads(v)
    scale = SB_DH ** -0.5
    outs = []
    for blk in range(S // Q_BLOCK):
        q0 = blk * Q_BLOCK
        end = q0 + Q_BLOCK
        qb = qh[:, :, q0:end]
        kb = kh[:, :, :end]
        vb = vh[:, :, :end]
        z = jnp.einsum('bhqd,bhkd->bhqk', qb, kb).astype(jnp.float32) * scale
        t_idx = q0 + jnp.arange(Q_BLOCK)[:, None]
        s_idx = jnp.arange(end)[None, :]
        past = s_idx < t_idx
        log_beta = jax.nn.log_sigmoid(z)
        log_1mb = jnp.where(past, jax.nn.log_sigmoid(-z), 0.0)
        after = lax.cumsum(log_1mb, axis=3, reverse=True) - log_1mb
        w = jnp.where(past, jnp.exp(log_beta + after), 0.0)
        outs.append(jnp.einsum('bhqk,bhkd->bhqd', w.astype(vb.dtype), vb))
    o = jnp.concatenate(outs, axis=2)
    return o.transpose(0, 2, 1, 3).reshape(B, S, SB_W)


def setup_inputs(seed: int = 0) -> dict:
    key = jax.random.key(seed)
    ks = jax.random.split(key, 16)
    L, D = DEPTH, D_MODEL

    def nrm(k, shape, fan_in):
        return jax.random.normal(k, shape, jnp.float32) * (fan_in ** -0.5)

    def gain(k, shape):
        return 1.0 + 0.05 * jax.random.normal(k, shape, jnp.float32)

    return {
        "x": jax.random.normal(ks[0], (BATCH, SEQ, D), jnp.float32),
        "c": jax.random.normal(ks[1], (BATCH, D), jnp.float32),
        "w_ada": nrm(ks[2], (L, D, 6 * D), D),
        "b_ada": 0.02 * jax.random.normal(ks[3], (L, 6 * D), jnp.float32),
        "norm_gains": gain(ks[4], (L, 4, D)),
        "w_in": nrm(ks[5], (L, D, IN_COLS), D),
        "w_gate_up": nrm(ks[6], (L, GLA_GATE_RANK, GLA_K), GLA_GATE_RANK),
        "b_gate": 0.1 * jax.random.normal(ks[7], (L, GLA_K), jnp.float32),
        "gla_norm_gain": gain(ks[8], (L, GLA_HEADS, GLA_DV)),
        "w_gla_o": nrm(ks[9], (L, GLA_V, D), GLA_V),
        "w_sb_o": nrm(ks[10], (L, SB_W, D), SB_W),
        "w_out": nrm(ks[11], (L, D, D), D),
        "w_ff1": nrm(ks[12], (L, D, D_FF), D),
        "w_ff2": nrm(ks[13], (L, D_FF, D), D_FF),
    }


def reference(x, c, w_ada, b_ada, norm_gains, w_in, w_gate_up, b_gate, gla_norm_gain,
              w_gla_o, w_sb_o, w_out, w_ff1, w_ff2):
    split_idx = [int(i) for i in np.cumsum(IN_SPLITS)[:-1]]
    c_act = jax.nn.silu(c)
    for l in range(DEPTH):
        mod = (c_act @ w_ada[l] + b_ada[l])[:, None, :]
        sh1, sc1, g1, sh2, sc2, g2 = jnp.split(mod, 6, axis=-1)
        ng = norm_gains[l]

        h = rmsnorm(x, ng[0]) * (1.0 + sc1) + sh1
        proj = h @ w_in[l]
        (q_a, k_a, v_a, r_a, a_low, q_b, k_b, v_b,
         gate_a, gate_b) = jnp.split(proj, split_idx, axis=-1)
        y_a = gla_branch(q_a, k_a, v_a, r_a, a_low, w_gate_up[l], b_gate[l], gla_norm_gain[l]) @ w_gla_o[l]
        y_b = sb_branch(q_b, k_b, v_b) @ w_sb_o[l]
        mixed = jax.nn.sigmoid(gate_a) * y_a + jax.nn.sigmoid(gate_b) * y_b
        x = x + g1 * rmsnorm(mixed @ w_out[l], ng[1])

        h = rmsnorm(x, ng[2]) * (1.0 + sc2) + sh2
        f = jnp.square(jax.nn.relu(h @ w_ff1[l])) @ w_ff2[l]
        x = x + g2 * rmsnorm(f, ng[3])
    return x
```

```python
import os
import numpy as np
import ml_dtypes
from contextlib import ExitStack
import concourse.bass as bass
import concourse.mybir as mybir
from concourse.bass_utils import run_bass_kernel_spmd

F32 = mybir.dt.float32
BF16 = mybir.dt.bfloat16
AF = mybir.ActivationFunctionType
ALU = mybir.AluOpType

COMPUTE = ("pe", "act", "dve", "pool")
ALL_ENG = ("pe", "act", "dve", "pool", "sp")
N_DMA_SEMS = 24
N_CC_SEMS = 4
CC_INC = 1

D = 2048
T = 1024
KC = 16
DEPTH = 4
EPS = 1e-6
IN_COLS = 16400
C_QA, C_KA, C_VA, C_RA, C_AL, C_QB, C_KB, C_VB, C_GA, C_GB = (
    0, 1024, 2048, 4096, 6144, 6160, 8208, 10256, 12304, 14352)


class _Ins:
    __slots__ = ("eng", "fn", "waits", "signal", "ext", "idx", "clock")

    def __init__(self, eng, fn):
        self.eng = eng
        self.fn = fn
        self.waits = []
        self.signal = False
        self.ext = None
        self.idx = -1
        self.clock = None


def _ap_range(ap):
    t = ap.tensor
    shape = list(t.shape)
    rsz = 1
    for s in shape[1:]:
        rsz *= s
    esz = mybir.dt.size(ap.dtype)
    off = int(ap.offset)
    r_lo = r_hi = off // rsz
    c_lo = c_hi = off % rsz
    for step, cnt in ap.ap:
        step = int(step)
        cnt = int(cnt)
        if cnt <= 1 or step == 0:
            continue
        ext = step * (cnt - 1)
        if abs(step) >= rsz and step % rsz == 0:
            e = ext // rsz
            if e > 0:
                r_hi += e
            else:
                r_lo += e
        else:
            if ext > 0:
                c_hi += ext
            else:
                c_lo += ext
    return t.name, r_lo, r_hi + 1, c_lo * esz, (c_hi + 1) * esz


class Prog:
    def __init__(self, nc):
        self.nc = nc
        self.ins = {e: [] for e in ALL_ENG}
        self.regs = {}
        self.known = {e: {} for e in ALL_ENG}
        self.known_ext = {e: set() for e in ALL_ENG}
        self.exts = []
        self.slot_last = {"d": [None] * N_DMA_SEMS, "c": [None] * N_CC_SEMS}
        self.slot_cnt = {"d": [0] * N_DMA_SEMS, "c": [0] * N_CC_SEMS}
        self.next_slot = {"d": 0, "c": 0}
        self.out_exts = []

    def _add_dep(self, ins, dep, raw):
        if dep is None:
            return
        eng = ins.eng
        if dep[0] == "x":
            if dep[1] in self.known_ext[eng]:
                return
            ins.waits.append(dep)
            return
        _, e2, idx = dep
        if e2 == eng:
            if not raw or eng == "pe":
                return
            if ins.idx - idx > 2:
                return
            ins.waits.append(dep)
            return
        if self.known[eng].get(e2, -1) >= idx:
            return
        ins.waits.append(dep)

    def add(self, eng, fn, reads=(), writes=(), ext=None, out=False):
        ins = _Ins(eng, fn)
        ins.idx = len(self.ins[eng])
        me = ("e", eng, ins.idx)
        prev_wait = None
        if ext is not None:
            kind = ext
            n = N_DMA_SEMS if kind == "d" else N_CC_SEMS
            xid = len(self.exts)
            slot = self.next_slot[kind]
            self.next_slot[kind] = (slot + 1) % n
            prev = self.slot_last[kind][slot]
            if prev is not None and prev not in self.known_ext[eng]:
                prev_wait = ("x", prev)
            self.slot_cnt[kind][slot] += 16 if kind == "d" else CC_INC
            self.exts.append((kind, slot, self.slot_cnt[kind][slot]))
            self.slot_last[kind][slot] = xid
            ins.ext = xid
            me = ("x", xid)
            if out:
                self.out_exts.append(xid)
        rr = [_ap_range(a) for a in reads]
        wr = [_ap_range(a) for a in writes]
        for (nm, r0, r1, c0, c1) in rr:
            for ent in self.regs.get(nm, ()):
                if ent[0] < r1 and r0 < ent[1] and ent[2] < c1 and c0 < ent[3]:
                    self._add_dep(ins, ent[4], True)
        for (nm, r0, r1, c0, c1) in wr:
            for ent in self.regs.get(nm, ()):
                if ent[0] < r1 and r0 < ent[1] and ent[2] < c1 and c0 < ent[3]:
                    self._add_dep(ins, ent[4], False)
                    for rk, rv in ent[5].items():
                        if isinstance(rk, tuple):
                            self._add_dep(ins, rk, False)
                        else:
                            self._add_dep(ins, ("e", rk, rv), False)
        best = {}
        xs = []
        for w in ins.waits:
            if w[0] == "e":
                if best.get(w[1], -1) < w[2]:
                    best[w[1]] = w[2]
            elif w not in xs:
                xs.append(w)
        if prev_wait is not None and prev_wait not in xs:
            xs.append(prev_wait)
        ins.waits = [("e", e, i) for e, i in best.items()] + xs
        kn = self.known[eng]
        kx = self.known_ext[eng]
        for w in ins.waits:
            if w[0] == "e":
                _, e2, i2 = w
                src = self.ins[e2][i2]
                src.signal = True
                if e2 != eng:
                    if kn.get(e2, -1) < i2:
                        kn[e2] = i2
                    ck = src.clock
                    for e3, i3 in ck[0].items():
                        if e3 != eng and kn.get(e3, -1) < i3:
                            kn[e3] = i3
                    kx |= ck[1]
            else:
                kx.add(w[1])
        ins.clock = (dict(kn), set(kx))
        for (nm, r0, r1, c0, c1) in rr:
            lst = self.regs.setdefault(nm, [])
            hit = None
            for ent in lst:
                if ent[0] <= r0 and r1 <= ent[1] and ent[2] <= c0 and c1 <= ent[3]:
                    hit = ent
                    break
            if hit is None:
                hit = [r0, r1, c0, c1, None, {}]
                lst.append(hit)
            if me[0] == "x":
                hit[5][me] = True
            else:
                hit[5][eng] = ins.idx
        for (nm, r0, r1, c0, c1) in wr:
            lst = self.regs.setdefault(nm, [])
            keep = [ent for ent in lst
                    if not (r0 <= ent[0] and ent[1] <= r1 and c0 <= ent[2] and ent[3] <= c1)]
            keep.append([r0, r1, c0, c1, me, {}])
            self.regs[nm] = keep
        self.ins[eng].append(ins)
        return ins

    def matmul(self, out, lhsT, rhs, start=True, stop=True):
        return self.add("pe", lambda e: e.matmul(out, lhsT, rhs, start=start, stop=stop),
                        reads=[lhsT, rhs], writes=[out])

    def transpose(self, out, in_, ident):
        return self.add("pe", lambda e: e.transpose(out, in_, ident), reads=[in_, ident], writes=[out])

    def activation(self, out, in_, func, bias=None, scale=None, accum_out=None, eng="act"):
        kw = {}
        rd = [in_]
        wr = [out]
        if bias is not None:
            kw["bias"] = bias
            if not isinstance(bias, (int, float)):
                rd.append(bias)
        if scale is not None:
            kw["scale"] = scale
            if not isinstance(scale, (int, float)):
                rd.append(scale)
        if accum_out is not None:
            kw["accum_out"] = accum_out
            wr.append(accum_out)
        return self.add(eng, lambda e: e.activation(out=out, in_=in_, func=func, **kw), reads=rd, writes=wr)

    def tt(self, out, in0, in1, op, eng="dve"):
        return self.add(eng, lambda e: e.tensor_tensor(out=out, in0=in0, in1=in1, op=op),
                        reads=[in0, in1], writes=[out])

    def ts(self, out, in0, s1, s2, op0, op1=None, eng="dve"):
        rd = [in0] + [s for s in (s1, s2) if s is not None and not isinstance(s, (int, float))]
        if op1 is None:
            return self.add(eng, lambda e: e.tensor_scalar(out=out, in0=in0, scalar1=s1, scalar2=None, op0=op0),
                            reads=rd, writes=[out])
        return self.add(eng, lambda e: e.tensor_scalar(out=out, in0=in0, scalar1=s1, scalar2=s2, op0=op0, op1=op1),
                        reads=rd, writes=[out])

    def stt(self, out, in0, scalar, in1, op0, op1, eng="dve"):
        rd = [in0, in1] + ([] if isinstance(scalar, (int, float)) else [scalar])
        return self.add(eng, lambda e: e.scalar_tensor_tensor(out=out, in0=in0, scalar=scalar, in1=in1, op0=op0, op1=op1),
                        reads=rd, writes=[out])

    def copy(self, out, in_, eng="dve"):
        if eng == "act":
            return self.activation(out, in_, AF.Copy)
        return self.add(eng, lambda e: e.tensor_copy(out=out, in_=in_), reads=[in_], writes=[out])

    def scan(self, out, data, initial, eng="dve"):
        rd = [data] + ([] if isinstance(initial, (int, float)) else [initial])
        return self.add(eng, lambda e: e.tensor_tensor_scan(out=out, data0=data, data1=data, initial=initial,
                                                          op0=ALU.add, op1=ALU.bypass), reads=rd, writes=[out])

    def rsqrt(self, out, in_, eps, scale_in=1.0):
        self.activation(out, in_, AF.Ln, bias=eps, scale=scale_in)
        return self.activation(out, out, AF.Exp, scale=-0.5)

    def memset(self, out, val, eng="dve"):
        return self.add(eng, lambda e: e.memset(out, val), writes=[out])

    def dma(self, out, in_, q="sp", final=False):
        return self.add(q, lambda e: e.dma_start(out=out, in_=in_), reads=[in_], writes=[out], ext="d", out=final)

    def allgather(self, out, in_, groups):
        return self.add("pool", lambda e: e.collective_compute("AllGather", ALU.bypass, replica_groups=groups,
                                                               ins=[in_], outs=[out]),
                        reads=[in_], writes=[out], ext="c")

    def emit(self):
        nc = self.nc
        with ExitStack() as st:
            esem = {e: st.enter_context(nc.semaphore("s_" + e)) for e in COMPUTE}
            xsem = {"d": [st.enter_context(nc.semaphore("d_%d" % i)) for i in range(N_DMA_SEMS)],
                    "c": [st.enter_context(nc.semaphore("c_%d" % i)) for i in range(N_CC_SEMS)]}
            cnt = {}
            for e in COMPUTE:
                c = 0
                arr = []
                for ins in self.ins[e]:
                    if ins.signal:
                        c += 1
                    arr.append(c)
                cnt[e] = arr

            def run(ename, h):
                for ins in self.ins[ename]:
                    for w in ins.waits:
                        if w[0] == "e":
                            h.wait_ge(esem[w[1]], cnt[w[1]][w[2]])
                        else:
                            kind, slot, val = self.exts[w[1]]
                            h.wait_ge(xsem[kind][slot], val)
                    bi = ins.fn(h)
                    if ins.ext is not None:
                        kind, slot, val = self.exts[ins.ext]
                        bi.then_inc(xsem[kind][slot], 16 if kind == "d" else CC_INC)
                    elif ins.signal:
                        bi.then_inc(esem[ename], 1)
                if ename == "sp":
                    for xid in self.out_exts:
                        kind, slot, val = self.exts[xid]
                        h.wait_ge(xsem[kind][slot], val)

            with nc.Block() as block:
                @block.tensor
                def _(h):
                    run("pe", h)

                @block.scalar
                def _(h):
                    run("act", h)

                @block.vector
                def _(h):
                    run("dve", h)

                @block.gpsimd
                def _(h):
                    run("pool", h)

                @block.sync
                def _(h):
                    run("sp", h)


_DSZ = {}


def build_program(n_cores=8, depth=DEPTH, dbg=None, stop_after=None):
    dbg = dbg or set()
    nc = bass.Bass("TRN2", target_bir_lowering=False)
    groups = [[2 * i, 2 * i + 1] for i in range(n_cores // 2)]
    P = Prog(nc)

    def din(name, shape, dt=F32):
        return nc.dram_tensor(name, list(shape), dt, kind="ExternalInput").ap()

    def dout(name, shape, dt=F32):
        return nc.dram_tensor(name, list(shape), dt, kind="ExternalOutput").ap()

    def dscr(name, shape, dt):
        return nc.dram_tensor(name, list(shape), dt).ap()

    x_in = din("x", [2 * T, D])
    cT_in = din("cT", [128, KC])
    w_ada = din("w_ada", [depth, D, 6 * D])
    b_adaT = din("b_adaT", [depth, 128, 96])
    gainsT = din("gainsT", [depth, 128, 4 * KC])
    w_in = din("w_in", [depth, D, IN_COLS])
    w_gu = din("w_gate_up", [depth, 16, 1024])
    b_gate = din("b_gate", [depth, 1, 1024])
    gn_bc_in = din("gn_bc", [depth, 128, 2048])
    w_gla_o = din("w_gla_o", [depth, D, D])
    w_sb_o = din("w_sb_o", [depth, D, D])
    w_out = din("w_out", [depth, D, D])
    w_ff1 = din("w_ff1", [depth, D, 4 * D])
    w_ff2 = din("w_ff2", [depth, 4 * D, D])
    consts_in = din("consts", [128, 1024])
    y_out = dout("y", [2 * T, D])

    xT_dd = [dscr("xT_d%d" % i, [D, T], F32) for i in range(2)]
    kT_dd = [dscr("kT_d%d" % i, [D, T], BF16) for i in range(2)]
    v_dd = [dscr("v_d%d" % i, [T, D], BF16) for i in range(2)]
    st_d = dscr("st_d", [1024, 512], F32)
    kdE_d = dscr("kdE_d", [T, 1024], BF16)
    kdO_d = dscr("kdO_d", [T, 1024], BF16)
    va_d = dscr("va_d", [T, D], BF16)
    mixT_d = dscr("mixT_d", [D, T], BF16)

    SCALE_SB = 128.0 ** -0.5

    with ExitStack() as st:
        def sb(name, shape, dt):
            return st.enter_context(nc.sbuf_tensor(name, list(shape), dt))

        ps = [st.enter_context(nc.psum_tensor("ps%d" % i, [128, 512], F32)) for i in range(8)]

        cst = sb("cst", [128, 1024], F32)
        ident_f = cst[:, 0:128]
        maskf = cst[:, 128:256]
        rowE = cst[:, 256:257]
        rowO = cst[:, 257:258]
        cstb = sb("cstb", [128, 768], BF16)
        ident_b = cstb[:, 0:128]
        ones_b = cstb[:, 128:256]
        m1_b = cstb[:, 256:384]
        csel_b = cstb[:, 384:386]
        one_row = cstb[0:1, 512:640]
        cact = sb("cact", [128, KC], BF16)
        modT = sb("modT", [128, 96], F32)
        badT = sb("badT", [128, 96], F32)
        gains = sb("gains", [128, 4 * KC], F32)
        coef = sb("coef", [128, 6 * KC], F32)
        smallf = sb("smallf", [128, 64], F32)
        dec = sb("dec", [128, 128], F32)
        carry = sb("carry", [128, 8], F32)
        alT = sb("alT", [16, 1024], BF16)
        wgu_sb = sb("wgu_sb", [16, 1024], BF16)
        bg_sb = sb("bg_sb", [1, 1024], BF16)
        gn_bc = sb("gn_bc_sb", [128, 2048], F32)
        rstd_bc = sb("rstd_bc", [128, 2, 512], F32)
        mixstg = sb("mixstg", [128, 2, 512], BF16)
        wbuf = [sb("wbuf%d" % i, [128, KC, 512], BF16) for i in range(3)]
        RA = sb("RA", [128, KC, T], BF16)
        RBC = sb("RBC", [128, KC, T], F32)
        RD = sb("RD", [128, KC, T], BF16)
        RB = RBC[:, 0:8, :].bitcast(BF16).rearrange("p a (b c) -> p (a b) c", c=T)
        RC = RBC[:, 8:16, :].bitcast(BF16).rearrange("p a (b c) -> p (a b) c", c=T)
        RDf = RD[:].bitcast(F32)

        def rdf(slot):
            return RDf[:, 2 * slot:2 * slot + 2, :].rearrange("p a b -> p (a b)")

        def rdb(slot):
            return RD[:, slot, :]

        st_w = {"i": 0}

        def wload(src2d, kc_n, ncols):
            wb = wbuf[st_w["i"] % 3]
            st_w["i"] += 1
            srcv = src2d.rearrange("(kc p) n -> p kc n", p=128)
            step = 4 if ncols >= 256 else kc_n
            for k0 in range(0, kc_n, step):
                k1 = min(kc_n, k0 + step)
                P.dma(wb[:, k0:k1, 0:ncols], srcv[:, k0:k1, :], q="pool")
            return wb

        lin_bank = {"i": 0}

        def next_bank():
            b = ps[lin_bank["i"] % 3]
            lin_bank["i"] += 1
            return b

        def linear_fm(wsrc, in_sb, kc_n, ncols, evac, blk=512):
            for c0 in range(0, ncols, blk):
                cw = min(blk, ncols - c0)
                wb = wload(wsrc[:, c0:c0 + cw], kc_n, cw)
                for j0 in range(0, cw, 128):
                    msz = min(128, cw - j0)
                    for th in range(2):
                        bank = next_bank()
                        for kc in range(kc_n):
                            P.matmul(bank[0:msz, :], wb[:, kc, j0:j0 + msz], in_sb[:, kc, th * 512:(th + 1) * 512],
                                     start=(kc == 0), stop=(kc == kc_n - 1))
                        evac((c0 + j0) // 128, th, bank[0:msz, :], msz)

        def linear_tm(wsrc, in_sb, kc_n, ncols, evac):
            for c0 in range(0, ncols, 512):
                cw = min(512, ncols - c0)
                wb = wload(wsrc[:, c0:c0 + cw], kc_n, cw)
                for tt in range(8):
                    bank = next_bank()
                    for kc in range(kc_n):
                        P.matmul(bank[:, 0:cw], in_sb[:, kc, tt * 128:(tt + 1) * 128], wb[:, kc, 0:cw],
                                 start=(kc == 0), stop=(kc == kc_n - 1))
                    evac(tt, c0, bank[:, 0:cw], cw)

        def dump(name, ap_src, dt=F32):
            if name not in dbg:
                return
            o = dout("dbg_" + name, list(ap_src.shape), dt)
            P.dma(o, ap_src, final=True)

        P.dma(cst[:], consts_in)
        P.copy(cstb[:, 0:128], cst[:, 0:128])
        P.copy(cstb[:, 128:768], cst[:, 384:1024])
        P.dma(smallf[:, 0:KC], cT_in)
        P.activation(cact[:], smallf[:, 0:KC], AF.Silu)
        for tt16 in range(16):
            xT_d = xT_dd[tt16 // 8]
            tt = tt16 % 8
            xt = rdf(tt % 2)
            xt2 = rdf(2 + tt % 2)
            P.dma(xt, x_in[tt16 * 128:(tt16 + 1) * 128, 0:1024])
            P.dma(xt2, x_in[tt16 * 128:(tt16 + 1) * 128, 1024:2048])
            for half, src in enumerate((xt, xt2)):
                for g in range(2):
                    bank = next_bank()
                    for j in range(4):
                        P.transpose(bank[:, j * 128:(j + 1) * 128], src[:, (g * 4 + j) * 128:(g * 4 + j + 1) * 128], ident_f)
                    o = rdf(4 + (half * 2 + g) % 2)
                    P.copy(o[:, 0:512], bank[:], eng=("act" if g else "dve"))
                    kc0 = half * 8 + g * 4
                    P.dma(xT_d[kc0 * 128:(kc0 + 4) * 128, tt * 128:(tt + 1) * 128].rearrange("(j p) t -> p j t", p=128),
                          o[:, 0:512].rearrange("p (j t) -> p j t", j=4))

        def sumsq_bank(th):
            return ps[4 + th]

        def norm_mod(a_col, sh_col, dst, xT_d):
            for kc in range(KC):
                xs = rdf(kc % 3)
                P.dma(xs, xT_d[kc * 128:(kc + 1) * 128, :])
                for th in range(2):
                    sq = rdb(12 + (kc * 2 + th) % 4)
                    P.activation(sq[:, 0:512], xs[:, th * 512:(th + 1) * 512], AF.Square)
                    P.matmul(sumsq_bank(th)[:], ones_b, sq[:, 0:512], start=(kc == 0), stop=(kc == KC - 1))
            for th in range(2):
                P.rsqrt(rstd_bc[:, th, :], sumsq_bank(th)[:], EPS)
            for kc in range(KC):
                xs = rdf(kc % 3)
                P.dma(xs, xT_d[kc * 128:(kc + 1) * 128, :])
                t1 = rdf(3 + kc % 2)
                for th in range(2):
                    P.stt(t1[:, th * 512:(th + 1) * 512], xs[:, th * 512:(th + 1) * 512],
                          coef[:, a_col + kc:a_col + kc + 1], rstd_bc[:, th, :], ALU.mult, ALU.mult)
                P.activation(dst[:, kc, :], t1, AF.Identity, bias=coef[:, sh_col + kc:sh_col + kc + 1])

        def sumsq_accum(src_f32, n, th):
            sq = rdb(12 + (n * 2 + th) % 4)
            P.activation(sq[:, 0:512], src_f32, AF.Square)
            P.matmul(sumsq_bank(th)[:], ones_b, sq[:, 0:512], start=(n == 0), stop=(n == KC - 1))

        def resid_update(yT, g_col, xT_d):
            for th in range(2):
                P.rsqrt(rstd_bc[:, th, :], sumsq_bank(th)[:], EPS)
            for kc in range(KC):
                xs = rdf(kc % 3)
                P.dma(xs, xT_d[kc * 128:(kc + 1) * 128, :])
                t1 = rdf(3 + kc % 2)
                for th in range(2):
                    P.stt(t1[:, th * 512:(th + 1) * 512], yT[:, kc, th * 512:(th + 1) * 512],
                          coef[:, g_col + kc:g_col + kc + 1], rstd_bc[:, th, :], ALU.mult, ALU.mult)
                P.tt(xs, xs, t1, ALU.add)
                P.dma(xT_d[kc * 128:(kc + 1) * 128, :], xs)

        def layer_half(l, half):
            xT_d = xT_dd[half]
            kT_in = kT_dd[half]
            v_in = v_dd[half]
            norm_mod(0, 16, RA, xT_d)
            if l == 0:
                dump("hT%d" % half, RA[:], BF16)
            if stop_after == "h":
                return True

            def ev_k(n, th, pa, msz):
                s = rdb(10 + n % 2)
                P.copy(s[:, th * 512:(th + 1) * 512], pa, eng=("act" if th else "dve"))
                if th == 1:
                    P.dma(kT_in[n * 128:(n + 1) * 128, :], s)
            linear_fm(w_in[l][:, C_KB:C_KB + D], RA, KC, D, ev_k)

            vi = {"i": 0}

            def ev_v(tt, c0, pa, cw):
                s = rdb(10 + vi["i"] % 4)
                vi["i"] += 1
                P.copy(s[:, 0:cw], pa, eng=("act" if tt % 2 else "dve"))
                P.dma(v_in[tt * 128:(tt + 1) * 128, c0:c0 + cw], s[:, 0:cw])
            linear_tm(w_in[l][:, C_VB:C_VB + D], RA, KC, D, ev_v)
            if stop_after == "a":
                return True

            kdE = RB[:, 0:8, :]
            kdO = RB[:, 8:16, :]
            va = RC.rearrange("p (a b) c -> p a (b c)", b=2)
            egb = RD[:, 0:8, :]

            def ev_al(n, th, pa, msz):
                P.copy(alT[0:16, th * 512:(th + 1) * 512], pa)
            linear_fm(w_in[l][:, C_AL:C_AL + 16], RA, KC, 16, ev_al)
            for tt in range(8):
                en = rdf(4 + tt % 2)
                spb = rdb(12 + tt % 2)
                for blk in range(2):
                    pb = ps[4 + blk]
                    P.matmul(pb[:], alT[0:16, tt * 128:(tt + 1) * 128], wgu_sb[0:16, blk * 512:(blk + 1) * 512],
                             start=True, stop=False)
                    P.matmul(pb[:], one_row, bg_sb[0:1, blk * 512:(blk + 1) * 512], start=False, stop=True)
                    P.activation(en[:, blk * 512:(blk + 1) * 512], pb[:], AF.Exp, scale=-1.0)
                    P.activation(spb[:, blk * 512:(blk + 1) * 512], en[:, blk * 512:(blk + 1) * 512], AF.Ln, bias=1.0)
                for blk in range(2):
                    gb = ps[4 + blk]
                    P.matmul(gb[:], m1_b, spb[:, blk * 512:(blk + 1) * 512])
                    P.activation(egb[:, tt, blk * 512:(blk + 1) * 512], gb[:], AF.Exp)
                for fc in range(8):
                    col = (tt * 8 + fc) * 2
                    P.matmul(ps[3][:, col:col + 2], spb[:, fc * 128:(fc + 1) * 128], csel_b)
            P.activation(dec[:], ps[3][:, 0:128], AF.Exp)

            def ev_ka(tt, c0, pa, cw):
                P.stt(kdE[:, tt, c0:c0 + cw], pa, rowE, egb[:, tt, c0:c0 + cw], ALU.mult, ALU.mult)
                P.stt(kdO[:, tt, c0:c0 + cw], pa, rowO, egb[:, tt, c0:c0 + cw], ALU.mult, ALU.mult)
                P.dma(kdE_d[tt * 128:(tt + 1) * 128, c0:c0 + cw], kdE[:, tt, c0:c0 + cw])
                P.dma(kdO_d[tt * 128:(tt + 1) * 128, c0:c0 + cw], kdO[:, tt, c0:c0 + cw])
            linear_tm(w_in[l][:, C_KA:C_KA + 1024], RA, KC, 1024, ev_ka)

            def ev_va(tt, c0, pa, cw):
                P.copy(va[:, tt, c0:c0 + cw], pa, eng=("act" if tt % 2 else "dve"))
                P.dma(va_d[tt * 128:(tt + 1) * 128, c0:c0 + cw], va[:, tt, c0:c0 + cw])
            linear_tm(w_in[l][:, C_VA:C_VA + D], RA, KC, D, ev_va)
            if stop_after == "b":
                return True

            qT4 = RD[:, 0:4, :]
            kvbuf = [[RD[:, 4 + 4 * s + i, :] for i in range(4)] for s in range(2)]
            RCf = RBC[:, 8:16, :].rearrange("p a (b c) -> p (a b) c", c=512)
            tset = [[RCf[:, 4 * s + i, :] for i in range(4)] for s in range(3)]
            wbf = [RCf[:, 12, :], RCf[:, 13, :], RDf[:, 14, :]]
            wT_sb = [RC[:, 14, :].rearrange("p (b q) -> p b q", b=2), RC[:, 15, :].rearrange("p (b q) -> p b q", b=2),
                     RD[:, 12, :].rearrange("p (b q) -> p b q", b=2), RD[:, 13, :].rearrange("p (b q) -> p b q", b=2)]
            tile_i = 0
            grp_i = 0
            SBV = int(os.environ.get("SBV", "6"))
            for hg in range(4):
                def ev_q(n, th, pa, msz):
                    P.copy(qT4[:, n % 4, th * 512:(th + 1) * 512], pa, eng=("act" if th else "dve"))
                linear_fm(w_in[l][:, C_QB + hg * 512:C_QB + (hg + 1) * 512], RA, KC, 512, ev_q)
                for hh in range(4):
                    h = hg * 4 + hh
                    kTo, kTp, vo_, vp_ = kvbuf[h % 2]
                    vo = vo_.rearrange("p (kb d) -> p kb d", d=128)
                    vp = vp_.rearrange("p (kb d) -> p kb d", d=128)
                    P.dma(kTo, kT_in[h * 128:(h + 1) * 128, :])
                    if half == 1:
                        P.dma(kTp, kT_dd[0][h * 128:(h + 1) * 128, :])
                    P.dma(vo, v_in[:, h * 128:(h + 1) * 128].rearrange("(kb p) d -> p kb d", p=128))
                    if half == 1:
                        P.dma(vp, v_dd[0][:, h * 128:(h + 1) * 128].rearrange("(kb p) d -> p kb d", p=128))
                    for qg in range(2):
                        chunks = [("o", qg, True)] + [("o", ci, False) for ci in reversed(range(qg))]
                        if half == 1:
                            chunks += [("p", 1, False), ("p", 0, False)]
                        oT = ps[3]
                        for cidx, (srcn, ci, diag) in enumerate(chunks):
                            kT = kTo if srcn == "o" else kTp
                            vv = vo if srcn == "o" else vp
                            wps = [ps[4], ps[5], ps[6], ps[7]]
                            wsb = [wT_sb[2 * (grp_i % 2)], wT_sb[2 * (grp_i % 2) + 1]]
                            grp_i += 1
                            for i in range(4):
                                qt = qg * 4 + i
                                wd = (i + 1) * 128 if diag else 512
                                e_, L_, C_, ec_ = tset[tile_i % 3]
                                wt = wbf[tile_i % 3]
                                zb = ps[tile_i % 3]
                                tile_i += 1
                                P.matmul(zb[:, 0:wd], qT4[:, hh, qt * 128:(qt + 1) * 128], kT[:, ci * 512:ci * 512 + wd])
                                P.activation(e_[:, 0:wd], zb[:, 0:wd], AF.Exp, scale=SCALE_SB)
                                P.activation(L_[:, 0:wd], e_[:, 0:wd], AF.Ln, bias=1.0)
                                if SBV < 2:
                                    continue
                                if diag:
                                    P.tt(L_[:, i * 128:(i + 1) * 128], L_[:, i * 128:(i + 1) * 128], maskf, ALU.mult)
                                    P.tt(e_[:, i * 128:(i + 1) * 128], e_[:, i * 128:(i + 1) * 128], maskf, ALU.mult)
                                P.scan(C_[:, 0:wd][:, ::-1], L_[:, 0:wd][:, ::-1],
                                       0.0 if diag else carry[:, qt:qt + 1])
                                P.copy(carry[:, qt:qt + 1], C_[:, 0:1])
                                if SBV < 3:
                                    continue
                                P.activation(ec_[:, 0:wd], C_[:, 0:wd], AF.Exp, scale=-1.0)
                                P.tt(wt[:, 0:wd], e_[:, 0:wd], ec_[:, 0:wd], ALU.mult)
                                for blk in range(wd // 128):
                                    P.transpose(wps[blk][:, i * 128:(i + 1) * 128],
                                                wt[:, blk * 128:(blk + 1) * 128], ident_f)
                            for blk in range(4):
                                if SBV < 4:
                                    continue
                                lo = blk * 128 if diag else 0
                                P.copy(wsb[blk // 2][:, blk % 2, lo:512], wps[blk][:, lo:512],
                                       eng=("act" if blk % 2 else "dve"))
                                if SBV >= 5:
                                    P.matmul(oT[:, lo:512], vv[:, ci * 4 + blk, :], wsb[blk // 2][:, blk % 2, lo:512],
                                             start=(cidx == 0 and blk == 0), stop=(cidx == len(chunks) - 1 and blk == 3))
                        if SBV >= 6:
                            P.copy(RB[:, h, qg * 512:(qg + 1) * 512], oT[:], eng="act")
            if l == 0:
                dump("sboT%d" % half, RB, BF16)
            if stop_after == "c":
                return True

            qE = RD[:, 0:2, :]
            qO = RD[:, 2:4, :]
            Sj = [RDf[:, 4, :], RDf[:, 5, :]]
            Sbf = [[RD[:, 12, 0:512], RD[:, 12, 512:1024]], [RD[:, 13, 0:512], RD[:, 13, 512:1024]]]
            strm = [(RD[:, 14, 0:256], RD[:, 14, 256:512], RD[:, 14, 512:1024]),
                    (RD[:, 15, 0:256], RD[:, 15, 256:512], RD[:, 15, 512:1024])]
            t1f = RDf[:, 8, :]
            rsb = RD[:, 6, 0:512]
            gbf = RDf[:, 9, :]
            ssum = smallf[:, 32:33]
            rs1 = smallf[:, 33:34]
            for qs in range(4):
                P.memset(RD[:, qs, :], 0.0, eng="pool")
            for h in range(4):
                for j in range(2):
                    if half == 0:
                        P.memset(Sj[j], 0.0, eng="pool")
                    else:
                        P.dma(Sj[j], st_d[(h * 2 + j) * 128:(h * 2 + j + 1) * 128, :])

                def ev_qa(n, th, pa, msz):
                    pv = pa.rearrange("p (a b c) -> p a b c", b=2, c=64)
                    qe = qE[:, n, th * 512:(th + 1) * 512].rearrange("p (a b c) -> p a b c", b=2, c=64)
                    qo = qO[:, n, th * 512:(th + 1) * 512].rearrange("p (a b c) -> p a b c", b=2, c=64)
                    P.activation(qe[:, :, 0, :], pv[:, :, 0, :], AF.Identity, scale=1.0 / 16.0)
                    P.activation(qo[:, :, 1, :], pv[:, :, 1, :], AF.Identity, scale=1.0 / 16.0)
                linear_fm(w_in[l][:, C_QA + h * 256:C_QA + (h + 1) * 256], RA, KC, 256, ev_qa, blk=256)
                wr = wload(w_in[l][:, C_RA + h * 512:C_RA + (h + 1) * 512], KC, 512)
                for tt in range(8):
                    kE_t, kO_t, va_t = strm[tt % 2]
                    P.dma(kE_t, kdE_d[tt * 128:(tt + 1) * 128, h * 256:(h + 1) * 256])
                    P.dma(kO_t, kdO_d[tt * 128:(tt + 1) * 128, h * 256:(h + 1) * 256])
                    P.dma(va_t, va_d[tt * 128:(tt + 1) * 128, h * 512:(h + 1) * 512])
                    rb = next_bank()
                    for kc in range(KC):
                        P.matmul(rb[:], RA[:, kc, tt * 128:(tt + 1) * 128], wr[:, kc, :], start=(kc == 0), stop=(kc == KC - 1))
                    P.activation(rsb, rb[:], AF.Silu)
                    for par in range(2):
                        kd_t = kE_t if par == 0 else kO_t
                        for j in range(2):
                            bank = ps[4 + j]
                            P.matmul(bank[:], kd_t[:, j * 128:(j + 1) * 128], va_t)
                            dcol = (tt * 8 + h * 2 + j) * 2 + par
                            P.stt(Sj[j], Sj[j], dec[:, dcol:dcol + 1], bank[:], ALU.mult, ALU.add)
                            P.copy(Sbf[par][j], Sj[j], eng="act")
                    ob = ps[3]
                    P.matmul(ob[:], qE[:, 0, tt * 128:(tt + 1) * 128], Sbf[0][0], start=True, stop=False)
                    P.matmul(ob[:], qE[:, 1, tt * 128:(tt + 1) * 128], Sbf[0][1], start=False, stop=False)
                    P.matmul(ob[:], qO[:, 0, tt * 128:(tt + 1) * 128], Sbf[1][0], start=False, stop=False)
                    P.matmul(ob[:], qO[:, 1, tt * 128:(tt + 1) * 128], Sbf[1][1], start=False, stop=True)
                    P.activation(t1f, ob[:], AF.Square)
                    P.scan(gbf, t1f, 0.0)
                    P.rsqrt(rs1, gbf[:, 511:512], EPS, 1.0 / 512.0)
                    P.stt(t1f, ob[:], rs1, gn_bc[:, h * 512:(h + 1) * 512], ALU.mult, ALU.mult)
                    P.tt(gbf, t1f, rsb, ALU.mult)
                    tb_ = ps[6]
                    for jj in range(4):
                        P.transpose(tb_[:, jj * 128:(jj + 1) * 128], gbf[:, jj * 128:(jj + 1) * 128], ident_f)
                    P.copy(RC[:, h * 4:(h + 1) * 4, tt * 128:(tt + 1) * 128],
                           tb_[:, 0:512].rearrange("p (a b) -> p a b", a=4), eng="act")
                if half == 0:
                    for j in range(2):
                        P.dma(st_d[(h * 2 + j) * 128:(h * 2 + j + 1) * 128, :], Sj[j])
            if l == 0:
                dump("glaoT%d" % half, RC, BF16)
            if stop_after == "d":
                return True

            tmpA = RDf[:, 0:4, :].rearrange("p (a b) c -> p a (b c)", b=2)
            tmpB = RDf[:, 4:8, :].rearrange("p (a b) c -> p a (b c)", b=2)
            for nb in range(8):
                def ev_ga(n, th, pa, msz):
                    P.activation(tmpA[:, n % 2, th * 512:(th + 1) * 512], pa, AF.Sigmoid)

                def ev_ya(n, th, pa, msz):
                    P.tt(tmpA[:, n % 2, th * 512:(th + 1) * 512], pa, tmpA[:, n % 2, th * 512:(th + 1) * 512], ALU.mult)

                def ev_gb(n, th, pa, msz):
                    P.activation(tmpB[:, n % 2, th * 512:(th + 1) * 512], pa, AF.Sigmoid)

                def ev_yb(n, th, pa, msz, nb=nb):
                    a = tmpA[:, n % 2, th * 512:(th + 1) * 512]
                    b_ = tmpB[:, n % 2, th * 512:(th + 1) * 512]
                    P.tt(b_, pa, b_, ALU.mult)
                    P.tt(a, a, b_, ALU.add)
                    mb = mixstg[:, (n * 2 + th) % 2, :]
                    P.copy(mb, a, eng="act")
                    P.dma(mixT_d[(nb * 2 + n % 2) * 128:(nb * 2 + n % 2 + 1) * 128, th * 512:(th + 1) * 512], mb)
                cs = slice(nb * 256, (nb + 1) * 256)
                linear_fm(w_in[l][:, C_GA + nb * 256:C_GA + (nb + 1) * 256], RA, KC, 256, ev_ga, blk=256)
                linear_fm(w_gla_o[l][:, cs], RC, KC, 256, ev_ya, blk=256)
                linear_fm(w_in[l][:, C_GB + nb * 256:C_GB + (nb + 1) * 256], RA, KC, 256, ev_gb, blk=256)
                linear_fm(w_sb_o[l][:, cs], RB, KC, 256, ev_yb, blk=256)
            if stop_after == "e":
                return True

            for kc in range(KC):
                P.dma(RA[:, kc, :], mixT_d[kc * 128:(kc + 1) * 128, :])
            if l == 0:
                dump("mixT%d" % half, RA[:], BF16)

            def ev_mo(n, th, pa, msz):
                P.copy(RBC[:, n, th * 512:(th + 1) * 512], pa, eng="dve")
                sumsq_accum(RBC[:, n, th * 512:(th + 1) * 512], n, th)
            linear_fm(w_out[l], RA, KC, D, ev_mo)
            resid_update(RBC, 32, xT_d)
            if stop_after == "f":
                return True

            norm_mod(48, 64, RA, xT_d)
            f1T = RD[:, 0:8, :]
            for fb in range(8):
                def ev_f1(n, th, pa, msz):
                    r_ = RDf[:, 8 + (n * 2 + th) % 2, :]
                    P.activation(r_, pa, AF.Relu)
                    P.tt(f1T[:, n % 8, th * 512:(th + 1) * 512], r_, r_, ALU.mult)
                linear_fm(w_ff1[l][:, fb * 1024:(fb + 1) * 1024], RA, KC, 1024, ev_f1)

                def ev_f2(n, th, pa, msz, fb=fb):
                    dst = RBC[:, n, th * 512:(th + 1) * 512]
                    if fb == 0:
                        P.copy(dst, pa, eng="act")
                    else:
                        P.tt(dst, dst, pa, ALU.add)
                    if fb == 7:
                        sumsq_accum(dst, n, th)
                linear_fm(w_ff2[l][fb * 1024:(fb + 1) * 1024, :], f1T, 8, D, ev_f2)
            resid_update(RBC, 80, xT_d)

            return False

        stop = False
        for l in range(depth):
            P.dma(gains[:], gainsT[l])
            P.dma(badT[:], b_adaT[l])
            P.dma(gn_bc[:], gn_bc_in[l])
            P.dma(wgu_sb[:], w_gu[l], q="pool")
            P.dma(bg_sb[:], b_gate[l], q="pool")
            pm = ps[3]
            for c0 in range(0, 6 * D, 512):
                wb = wload(w_ada[l][:, c0:c0 + 512], KC, 512)
                for j in range(4):
                    col = c0 // 128 + j
                    for kc in range(KC):
                        P.matmul(pm[:, col:col + 1], wb[:, kc, j * 128:(j + 1) * 128], cact[:, kc:kc + 1],
                                 start=(kc == 0), stop=(kc == KC - 1))
            P.tt(modT[:], pm[:, 0:96], badT[:], ALU.add)
            P.stt(coef[:, 0:16], modT[:, 16:32], 1.0, gains[:, 0:16], ALU.add, ALU.mult)
            P.copy(coef[:, 16:32], modT[:, 0:16])
            P.tt(coef[:, 32:48], modT[:, 32:48], gains[:, 16:32], ALU.mult)
            P.stt(coef[:, 48:64], modT[:, 64:80], 1.0, gains[:, 32:48], ALU.add, ALU.mult)
            P.copy(coef[:, 64:80], modT[:, 48:64])
            P.tt(coef[:, 80:96], modT[:, 80:96], gains[:, 48:64], ALU.mult)
            if l == 0:
                dump("modT", modT[:])

            for half in range(2):
                if layer_half(l, half):
                    stop = True
                    break
            if stop:
                break

        for tt16 in range(16):
            xT_d = xT_dd[tt16 // 8]
            tt = tt16 % 8
            for half in range(2):
                o = rdf((tt * 2 + half) % 2)
                for g in range(2):
                    kc0 = half * 8 + g * 4
                    src = rdf(2 + g)
                    P.dma(src[:, 0:512].rearrange("p (j t) -> p j t", j=4),
                          xT_d[kc0 * 128:(kc0 + 4) * 128, tt * 128:(tt + 1) * 128].rearrange("(j p) t -> p j t", p=128))
                    bank = next_bank()
                    for j in range(4):
                        P.transpose(bank[:, j * 128:(j + 1) * 128], src[:, j * 128:(j + 1) * 128], ident_f)
                    P.copy(o[:, g * 512:(g + 1) * 512], bank[:], eng=("act" if g else "dve"))
                P.dma(y_out[tt16 * 128:(tt16 + 1) * 128, half * 1024:(half + 1) * 1024], o, final=True)
        P.emit()
    return nc


def make_consts():
    c = np.zeros((128, 1024), np.float32)
    p = np.arange(128)[:, None]
    f = np.arange(128)[None, :]
    c[:, 0:128] = (p == f)
    c[:, 128:256] = (f < p)
    c[:, 256] = (np.arange(128) < 64)
    c[:, 257] = (np.arange(128) >= 64)
    c[:, 384:512] = 1.0 / 2048.0
    same = (p // 64) == (f // 64)
    c[:, 512:640] = np.where((p > f) & same, -1.0 / 16.0, 0.0)
    c[:, 640] = np.where(np.arange(128) < 64, -1.0 / 16.0, 0.0)
    c[:, 641] = np.where(np.arange(128) >= 64, -1.0 / 16.0, 0.0)
    c[:, 768:896] = 1.0
    return c


def make_in_maps(inputs, n_cores=8, depth=DEPTH):
    f32 = lambda a: np.ascontiguousarray(np.asarray(a, dtype=np.float32))
    x = f32(inputs["x"])
    c = f32(inputs["c"])
    L = DEPTH
    shared0 = {
        "w_ada": f32(inputs["w_ada"]),
        "w_in": f32(inputs["w_in"]),
        "w_gate_up": f32(inputs["w_gate_up"]),
        "w_gla_o": f32(inputs["w_gla_o"]),
        "w_sb_o": f32(inputs["w_sb_o"]),
        "w_out": f32(inputs["w_out"]),
        "w_ff1": f32(inputs["w_ff1"]),
        "w_ff2": f32(inputs["w_ff2"]),
        "b_adaT": np.ascontiguousarray(f32(inputs["b_ada"]).reshape(L, 96, 128).transpose(0, 2, 1)),
        "gainsT": np.ascontiguousarray(f32(inputs["norm_gains"]).reshape(L, 4, KC, 128).transpose(0, 3, 1, 2).reshape(L, 128, 4 * KC)),
        "b_gate": f32(inputs["b_gate"]).reshape(L, 1, 1024),
        "gn_bc": np.ascontiguousarray(np.broadcast_to(f32(inputs["gla_norm_gain"]).reshape(L, 1, 2048), (L, 128, 2048))),
    }
    shared = {k: np.ascontiguousarray(v[:depth]) for k, v in shared0.items()}
    shared["consts"] = make_consts()
    maps = []
    for core in range(n_cores):
        b = core % 4
        m = dict(shared)
        m["x"] = np.ascontiguousarray(x[b])
        m["cT"] = np.ascontiguousarray(c[b].reshape(KC, 128).T)
        maps.append(m)
    return maps


_NC_CACHE = {}
N_LAUNCH_CORES = 4


def kernel(**inputs):
    n = N_LAUNCH_CORES
    if "nc" not in _NC_CACHE:
        _NC_CACHE["nc"] = build_program(n)
    nc = _NC_CACHE["nc"]
    maps = make_in_maps(inputs, n)
    res = run_bass_kernel_spmd(nc, maps, core_ids=list(range(n)))
    out = np.empty((4, 2 * T, D), np.float32)
    for b in range(4):
        out[b] = np.asarray(res.results[b]["y"], dtype=np.float32)
    return out
```

```python
import os
import numpy as np
import ml_dtypes
from contextlib import ExitStack
import concourse.bass as bass
import concourse.mybir as mybir
from concourse.bass_utils import run_bass_kernel_spmd

F32 = mybir.dt.float32
BF16 = mybir.dt.bfloat16
AF = mybir.ActivationFunctionType
ALU = mybir.AluOpType

COMPUTE = ("pe", "act", "dve", "pool")
ALL_ENG = ("pe", "act", "dve", "pool", "sp")
N_DMA_SEMS = 24
N_CC_SEMS = 4
CC_INC = 1

D = 2048
T = 1024
KC = 16
DEPTH = 4
EPS = 1e-6
IN_COLS = 16400
C_QA, C_KA, C_VA, C_RA, C_AL, C_QB, C_KB, C_VB, C_GA, C_GB = (
    0, 1024, 2048, 4096, 6144, 6160, 8208, 10256, 12304, 14352)


class _Ins:
    __slots__ = ("eng", "fn", "waits", "signal", "ext", "idx", "clock")

    def __init__(self, eng, fn):
        self.eng = eng
        self.fn = fn
        self.waits = []
        self.signal = False
        self.ext = None
        self.idx = -1
        self.clock = None


def _ap_range(ap):
    t = ap.tensor
    shape = list(t.shape)
    rsz = 1
    for s in shape[1:]:
        rsz *= s
    esz = mybir.dt.size(ap.dtype)
    off = int(ap.offset)
    r_lo = r_hi = off // rsz
    c_lo = c_hi = off % rsz
    for step, cnt in ap.ap:
        step = int(step)
        cnt = int(cnt)
        if cnt <= 1 or step == 0:
            continue
        ext = step * (cnt - 1)
        if abs(step) >= rsz and step % rsz == 0:
            e = ext // rsz
            if e > 0:
                r_hi += e
            else:
                r_lo += e
        else:
            if ext > 0:
                c_hi += ext
            else:
                c_lo += ext
    return t.name, r_lo, r_hi + 1, c_lo * esz, (c_hi + 1) * esz


class Prog:
    def __init__(self, nc):
        self.nc = nc
        self.ins = {e: [] for e in ALL_ENG}
        self.regs = {}
        self.known = {e: {} for e in ALL_ENG}
        self.known_ext = {e: set() for e in ALL_ENG}
        self.exts = []
        self.slot_last = {"d": [None] * N_DMA_SEMS, "c": [None] * N_CC_SEMS}
        self.slot_cnt = {"d": [0] * N_DMA_SEMS, "c": [0] * N_CC_SEMS}
        self.next_slot = {"d": 0, "c": 0}
        self.out_exts = []

    def _add_dep(self, ins, dep, raw):
        if dep is None:
            return
        eng = ins.eng
        if dep[0] == "x":
            if dep[1] in self.known_ext[eng]:
                return
            ins.waits.append(dep)
            return
        _, e2, idx = dep
        if e2 == eng:
            if not raw or eng == "pe":
                return
            if ins.idx - idx > 2:
                return
            ins.waits.append(dep)
            return
        if self.known[eng].get(e2, -1) >= idx:
            return
        ins.waits.append(dep)

    def add(self, eng, fn, reads=(), writes=(), ext=None, out=False):
        ins = _Ins(eng, fn)
        ins.idx = len(self.ins[eng])
        me = ("e", eng, ins.idx)
        prev_wait = None
        if ext is not None:
            kind = ext
            n = N_DMA_SEMS if kind == "d" else N_CC_SEMS
            xid = len(self.exts)
            slot = self.next_slot[kind]
            self.next_slot[kind] = (slot + 1) % n
            prev = self.slot_last[kind][slot]
            if prev is not None and prev not in self.known_ext[eng]:
                prev_wait = ("x", prev)
            self.slot_cnt[kind][slot] += 16 if kind == "d" else CC_INC
            self.exts.append((kind, slot, self.slot_cnt[kind][slot]))
            self.slot_last[kind][slot] = xid
            ins.ext = xid
            me = ("x", xid)
            if out:
                self.out_exts.append(xid)
        rr = [_ap_range(a) for a in reads]
        wr = [_ap_range(a) for a in writes]
        for (nm, r0, r1, c0, c1) in rr:
            for ent in self.regs.get(nm, ()):
                if ent[0] < r1 and r0 < ent[1] and ent[2] < c1 and c0 < ent[3]:
                    self._add_dep(ins, ent[4], True)
        for (nm, r0, r1, c0, c1) in wr:
            for ent in self.regs.get(nm, ()):
                if ent[0] < r1 and r0 < ent[1] and ent[2] < c1 and c0 < ent[3]:
                    self._add_dep(ins, ent[4], False)
                    for rk, rv in ent[5].items():
                        if isinstance(rk, tuple):
                            self._add_dep(ins, rk, False)
                        else:
                            self._add_dep(ins, ("e", rk, rv), False)
        best = {}
        xs = []
        for w in ins.waits:
            if w[0] == "e":
                if best.get(w[1], -1) < w[2]:
                    best[w[1]] = w[2]
            elif w not in xs:
                xs.append(w)
        if prev_wait is not None and prev_wait not in xs:
            xs.append(prev_wait)
        ins.waits = [("e", e, i) for e, i in best.items()] + xs
        kn = self.known[eng]
        kx = self.known_ext[eng]
        for w in ins.waits:
            if w[0] == "e":
                _, e2, i2 = w
                src = self.ins[e2][i2]
                src.signal = True
                if e2 != eng:
                    if kn.get(e2, -1) < i2:
                        kn[e2] = i2
                    ck = src.clock
                    for e3, i3 in ck[0].items():
                        if e3 != eng and kn.get(e3, -1) < i3:
                            kn[e3] = i3
                    kx |= ck[1]
            else:
                kx.add(w[1])
        ins.clock = (dict(kn), set(kx))
        for (nm, r0, r1, c0, c1) in rr:
            lst = self.regs.setdefault(nm, [])
            hit = None
            for ent in lst:
                if ent[0] <= r0 and r1 <= ent[1] and ent[2] <= c0 and c1 <= ent[3]:
                    hit = ent
                    break
            if hit is None:
                hit = [r0, r1, c0, c1, None, {}]
                lst.append(hit)
            if me[0] == "x":
                hit[5][me] = True
            else:
                hit[5][eng] = ins.idx
        for (nm, r0, r1, c0, c1) in wr:
            lst = self.regs.setdefault(nm, [])
            keep = [ent for ent in lst
                    if not (r0 <= ent[0] and ent[1] <= r1 and c0 <= ent[2] and ent[3] <= c1)]
            keep.append([r0, r1, c0, c1, me, {}])
            self.regs[nm] = keep
        self.ins[eng].append(ins)
        return ins

    def matmul(self, out, lhsT, rhs, start=True, stop=True):
        return self.add("pe", lambda e: e.matmul(out, lhsT, rhs, start=start, stop=stop),
                        reads=[lhsT, rhs], writes=[out])

    def transpose(self, out, in_, ident):
        return self.add("pe", lambda e: e.transpose(out, in_, ident), reads=[in_, ident], writes=[out])

    def activation(self, out, in_, func, bias=None, scale=None, accum_out=None, eng="act"):
        kw = {}
        rd = [in_]
        wr = [out]
        if bias is not None:
            kw["bias"] = bias
            if not isinstance(bias, (int, float)):
                rd.append(bias)
        if scale is not None:
            kw["scale"] = scale
            if not isinstance(scale, (int, float)):
                rd.append(scale)
        if accum_out is not None:
            kw["accum_out"] = accum_out
            wr.append(accum_out)
        return self.add(eng, lambda e: e.activation(out=out, in_=in_, func=func, **kw), reads=rd, writes=wr)

    def tt(self, out, in0, in1, op, eng="dve"):
        return self.add(eng, lambda e: e.tensor_tensor(out=out, in0=in0, in1=in1, op=op),
                        reads=[in0, in1], writes=[out])

    def ts(self, out, in0, s1, s2, op0, op1=None, eng="dve"):
        rd = [in0] + [s for s in (s1, s2) if s is not None and not isinstance(s, (int, float))]
        if op1 is None:
            return self.add(eng, lambda e: e.tensor_scalar(out=out, in0=in0, scalar1=s1, scalar2=None, op0=op0),
                            reads=rd, writes=[out])
        return self.add(eng, lambda e: e.tensor_scalar(out=out, in0=in0, scalar1=s1, scalar2=s2, op0=op0, op1=op1),
                        reads=rd, writes=[out])

    def stt(self, out, in0, scalar, in1, op0, op1, eng="dve"):
        rd = [in0, in1] + ([] if isinstance(scalar, (int, float)) else [scalar])
        return self.add(eng, lambda e: e.scalar_tensor_tensor(out=out, in0=in0, scalar=scalar, in1=in1, op0=op0, op1=op1),
                        reads=rd, writes=[out])

    def copy(self, out, in_, eng="dve"):
        if eng == "act":
            return self.activation(out, in_, AF.Copy)
        return self.add(eng, lambda e: e.tensor_copy(out=out, in_=in_), reads=[in_], writes=[out])

    def scan(self, out, data, initial, eng="dve"):
        rd = [data] + ([] if isinstance(initial, (int, float)) else [initial])
        return self.add(eng, lambda e: e.tensor_tensor_scan(out=out, data0=data, data1=data, initial=initial,
                                                          op0=ALU.add, op1=ALU.bypass), reads=rd, writes=[out])

    def rsqrt(self, out, in_, eps, scale_in=1.0):
        self.activation(out, in_, AF.Ln, bias=eps, scale=scale_in)
        return self.activation(out, out, AF.Exp, scale=-0.5)

    def memset(self, out, val, eng="dve"):
        return self.add(eng, lambda e: e.memset(out, val), writes=[out])

    def dma(self, out, in_, q="sp", final=False):
        return self.add(q, lambda e: e.dma_start(out=out, in_=in_), reads=[in_], writes=[out], ext="d", out=final)

    def allgather(self, out, in_, groups):
        return self.add("pool", lambda e: e.collective_compute("AllGather", ALU.bypass, replica_groups=groups,
                                                               ins=[in_], outs=[out]),
                        reads=[in_], writes=[out], ext="c")

    def emit(self):
        nc = self.nc
        with ExitStack() as st:
            esem = {e: st.enter_context(nc.semaphore("s_" + e)) for e in COMPUTE}
            xsem = {"d": [st.enter_context(nc.semaphore("d_%d" % i)) for i in range(N_DMA_SEMS)],
                    "c": [st.enter_context(nc.semaphore("c_%d" % i)) for i in range(N_CC_SEMS)]}
            cnt = {}
            for e in COMPUTE:
                c = 0
                arr = []
                for ins in self.ins[e]:
                    if ins.signal:
                        c += 1
                    arr.append(c)
                cnt[e] = arr

            def run(ename, h):
                for ins in self.ins[ename]:
                    for w in ins.waits:
                        if w[0] == "e":
                            h.wait_ge(esem[w[1]], cnt[w[1]][w[2]])
                        else:
                            kind, slot, val = self.exts[w[1]]
                            h.wait_ge(xsem[kind][slot], val)
                    bi = ins.fn(h)
                    if ins.ext is not None:
                        kind, slot, val = self.exts[ins.ext]
                        bi.then_inc(xsem[kind][slot], 16 if kind == "d" else CC_INC)
                    elif ins.signal:
                        bi.then_inc(esem[ename], 1)
                if ename == "sp":
                    for xid in self.out_exts:
                        kind, slot, val = self.exts[xid]
                        h.wait_ge(xsem[kind][slot], val)

            with nc.Block() as block:
                @block.tensor
                def _(h):
                    run("pe", h)

                @block.scalar
                def _(h):
                    run("act", h)

                @block.vector
                def _(h):
                    run("dve", h)

                @block.gpsimd
                def _(h):
                    run("pool", h)

                @block.sync
                def _(h):
                    run("sp", h)


_DSZ = {}


def build_program(n_cores=8, depth=DEPTH, dbg=None, stop_after=None):
    dbg = dbg or set()
    nc = bass.Bass("TRN2", target_bir_lowering=False)
    groups = [[2 * i, 2 * i + 1] for i in range(n_cores // 2)]
    P = Prog(nc)

    def din(name, shape, dt=F32):
        return nc.dram_tensor(name, list(shape), dt, kind="ExternalInput").ap()

    def dout(name, shape, dt=F32):
        return nc.dram_tensor(name, list(shape), dt, kind="ExternalOutput").ap()

    def dscr(name, shape, dt):
        return nc.dram_tensor(name, list(shape), dt).ap()

    x_in = din("x", [2 * T, D])
    cT_in = din("cT", [128, KC])
    w_ada = din("w_ada", [depth, D, 6 * D])
    b_adaT = din("b_adaT", [depth, 128, 96])
    gainsT = din("gainsT", [depth, 128, 4 * KC])
    w_in = din("w_in", [depth, D, IN_COLS])
    w_gu = din("w_gate_up", [depth, 16, 1024])
    b_gate = din("b_gate", [depth, 1, 1024])
    gn_bc_in = din("gn_bc", [depth, 128, 2048])
    w_gla_o = din("w_gla_o", [depth, D, D])
    w_sb_o = din("w_sb_o", [depth, D, D])
    w_out = din("w_out", [depth, D, D])
    w_ff1 = din("w_ff1", [depth, D, 4 * D])
    w_ff2 = din("w_ff2", [depth, 4 * D, D])
    consts_in = din("consts", [128, 1024])
    y_out = dout("y", [2 * T, D])

    xT_dd = [dscr("xT_d%d" % i, [D, T], F32) for i in range(2)]
    kT_dd = [dscr("kT_d%d" % i, [D, T], BF16) for i in range(2)]
    v_dd = [dscr("v_d%d" % i, [T, D], BF16) for i in range(2)]
    st_d = dscr("st_d", [1024, 512], F32)
    kdE_d = dscr("kdE_d", [T, 1024], BF16)
    kdO_d = dscr("kdO_d", [T, 1024], BF16)
    va_d = dscr("va_d", [T, D], BF16)
    mixT_d = dscr("mixT_d", [D, T], BF16)

    SCALE_SB = 128.0 ** -0.5

    with ExitStack() as st:
        def sb(name, shape, dt):
            return st.enter_context(nc.sbuf_tensor(name, list(shape), dt))

        ps = [st.enter_context(nc.psum_tensor("ps%d" % i, [128, 512], F32)) for i in range(8)]

        cst = sb("cst", [128, 1024], F32)
        ident_f = cst[:, 0:128]
        maskf = cst[:, 128:256]
        rowE = cst[:, 256:257]
        rowO = cst[:, 257:258]
        cstb = sb("cstb", [128, 768], BF16)
        ident_b = cstb[:, 0:128]
        ones_b = cstb[:, 128:256]
        m1_b = cstb[:, 256:384]
        csel_b = cstb[:, 384:386]
        one_row = cstb[0:1, 512:640]
        cact = sb("cact", [128, KC], BF16)
        modT = sb("modT", [128, 96], F32)
        badT = sb("badT", [128, 96], F32)
        gains = sb("gains", [128, 4 * KC], F32)
        coef = sb("coef", [128, 6 * KC], F32)
        smallf = sb("smallf", [128, 64], F32)
        dec = sb("dec", [128, 128], F32)
        carry = sb("carry", [128, 8], F32)
        alT = sb("alT", [16, 1024], BF16)
        wgu_sb = sb("wgu_sb", [16, 1024], BF16)
        bg_sb = sb("bg_sb", [1, 1024], BF16)
        gn_bc = sb("gn_bc_sb", [128, 2048], F32)
        rstd_bc = sb("rstd_bc", [128, 2, 512], F32)
        mixstg = sb("mixstg", [128, 2, 512], BF16)
        wbuf = [sb("wbuf%d" % i, [128, KC, 512], BF16) for i in range(3)]
        RA = sb("RA", [128, KC, T], BF16)
        RBC = sb("RBC", [128, KC, T], F32)
        RD = sb("RD", [128, KC, T], BF16)
        RB = RBC[:, 0:8, :].bitcast(BF16).rearrange("p a (b c) -> p (a b) c", c=T)
        RC = RBC[:, 8:16, :].bitcast(BF16).rearrange("p a (b c) -> p (a b) c", c=T)
        RDf = RD[:].bitcast(F32)

        def rdf(slot):
            return RDf[:, 2 * slot:2 * slot + 2, :].rearrange("p a b -> p (a b)")

        def rdb(slot):
            return RD[:, slot, :]

        st_w = {"i": 0}

        def wload(src2d, kc_n, ncols):
            wb = wbuf[st_w["i"] % 3]
            st_w["i"] += 1
            srcv = src2d.rearrange("(kc p) n -> p kc n", p=128)
            step = 4 if ncols >= 256 else kc_n
            for k0 in range(0, kc_n, step):
                k1 = min(kc_n, k0 + step)
                P.dma(wb[:, k0:k1, 0:ncols], srcv[:, k0:k1, :], q="pool")
            return wb

        lin_bank = {"i": 0}

        def next_bank():
            b = ps[lin_bank["i"] % 3]
            lin_bank["i"] += 1
            return b

        def linear_fm(wsrc, in_sb, kc_n, ncols, evac, blk=512):
            for c0 in range(0, ncols, blk):
                cw = min(blk, ncols - c0)
                wb = wload(wsrc[:, c0:c0 + cw], kc_n, cw)
                for j0 in range(0, cw, 128):
                    msz = min(128, cw - j0)
                    for th in range(2):
                        bank = next_bank()
                        for kc in range(kc_n):
                            P.matmul(bank[0:msz, :], wb[:, kc, j0:j0 + msz], in_sb[:, kc, th * 512:(th + 1) * 512],
                                     start=(kc == 0), stop=(kc == kc_n - 1))
                        evac((c0 + j0) // 128, th, bank[0:msz, :], msz)

        def linear_tm(wsrc, in_sb, kc_n, ncols, evac):
            for c0 in range(0, ncols, 512):
                cw = min(512, ncols - c0)
                wb = wload(wsrc[:, c0:c0 + cw], kc_n, cw)
                for tt in range(8):
                    bank = next_bank()
                    for kc in range(kc_n):
                        P.matmul(bank[:, 0:cw], in_sb[:, kc, tt * 128:(tt + 1) * 128], wb[:, kc, 0:cw],
                                 start=(kc == 0), stop=(kc == kc_n - 1))
                    evac(tt, c0, bank[:, 0:cw], cw)

        def dump(name, ap_src, dt=F32):
            if name not in dbg:
                return
            o = dout("dbg_" + name, list(ap_src.shape), dt)
            P.dma(o, ap_src, final=True)

        P.dma(cst[:], consts_in)
        P.copy(cstb[:, 0:128], cst[:, 0:128])
        P.copy(cstb[:, 128:768], cst[:, 384:1024])
        P.dma(smallf[:, 0:KC], cT_in)
        P.activation(cact[:], smallf[:, 0:KC], AF.Silu)
        for tt16 in range(16):
            xT_d = xT_dd[tt16 // 8]
            tt = tt16 % 8
            xt = rdf(tt % 2)
            xt2 = rdf(2 + tt % 2)
            P.dma(xt, x_in[tt16 * 128:(tt16 + 1) * 128, 0:1024])
            P.dma(xt2, x_in[tt16 * 128:(tt16 + 1) * 128, 1024:2048])
            for half, src in enumerate((xt, xt2)):
                for g in range(2):
                    bank = next_bank()
                    for j in range(4):
                        P.transpose(bank[:, j * 128:(j + 1) * 128], src[:, (g * 4 + j) * 128:(g * 4 + j + 1) * 128], ident_f)
                    o = rdf(4 + (half * 2 + g) % 2)
                    P.copy(o[:, 0:512], bank[:], eng=("act" if g else "dve"))
                    kc0 = half * 8 + g * 4
                    P.dma(xT_d[kc0 * 128:(kc0 + 4) * 128, tt * 128:(tt + 1) * 128].rearrange("(j p) t -> p j t", p=128),
                          o[:, 0:512].rearrange("p (j t) -> p j t", j=4))

        def sumsq_bank(th):
            return ps[4 + th]

        def norm_mod(a_col, sh_col, dst, xT_d):
            for kc in range(KC):
                xs = rdf(kc % 3)
                P.dma(xs, xT_d[kc * 128:(kc + 1) * 128, :])
                for th in range(2):
                    sq = rdb(12 + (kc * 2 + th) % 4)
                    P.activation(sq[:, 0:512], xs[:, th * 512:(th + 1) * 512], AF.Square)
                    P.matmul(sumsq_bank(th)[:], ones_b, sq[:, 0:512], start=(kc == 0), stop=(kc == KC - 1))
            for th in range(2):
                P.rsqrt(rstd_bc[:, th, :], sumsq_bank(th)[:], EPS)
            for kc in range(KC):
                xs = rdf(kc % 3)
                P.dma(xs, xT_d[kc * 128:(kc + 1) * 128, :])
                t1 = rdf(3 + kc % 2)
                for th in range(2):
                    P.stt(t1[:, th * 512:(th + 1) * 512], xs[:, th * 512:(th + 1) * 512],
                          coef[:, a_col + kc:a_col + kc + 1], rstd_bc[:, th, :], ALU.mult, ALU.mult)
                P.activation(dst[:, kc, :], t1, AF.Identity, bias=coef[:, sh_col + kc:sh_col + kc + 1])

        def sumsq_accum(src_f32, n, th):
            sq = rdb(12 + (n * 2 + th) % 4)
            P.activation(sq[:, 0:512], src_f32, AF.Square)
            P.matmul(sumsq_bank(th)[:], ones_b, sq[:, 0:512], start=(n == 0), stop=(n == KC - 1))

        def resid_update(yT, g_col, xT_d):
            for th in range(2):
                P.rsqrt(rstd_bc[:, th, :], sumsq_bank(th)[:], EPS)
            for kc in range(KC):
                xs = rdf(kc % 3)
                P.dma(xs, xT_d[kc * 128:(kc + 1) * 128, :])
                t1 = rdf(3 + kc % 2)
                for th in range(2):
                    P.stt(t1[:, th * 512:(th + 1) * 512], yT[:, kc, th * 512:(th + 1) * 512],
                          coef[:, g_col + kc:g_col + kc + 1], rstd_bc[:, th, :], ALU.mult, ALU.mult)
                P.tt(xs, xs, t1, ALU.add)
                P.dma(xT_d[kc * 128:(kc + 1) * 128, :], xs)

        def layer_half(l, half):
            xT_d = xT_dd[half]
            kT_in = kT_dd[half]
            v_in = v_dd[half]
            norm_mod(0, 16, RA, xT_d)
            if l == 0:
                dump("hT%d" % half, RA[:], BF16)
            if stop_after == "h":
                return True

            def ev_k(n, th, pa, msz):
                s = rdb(10 + n % 2)
                P.copy(s[:, th * 512:(th + 1) * 512], pa, eng=("act" if th else "dve"))
                if th == 1:
                    P.dma(kT_in[n * 128:(n + 1) * 128, :], s)
            linear_fm(w_in[l][:, C_KB:C_KB + D], RA, KC, D, ev_k)

            vi = {"i": 0}

            def ev_v(tt, c0, pa, cw):
                s = rdb(10 + vi["i"] % 4)
                vi["i"] += 1
                P.copy(s[:, 0:cw], pa, eng=("act" if tt % 2 else "dve"))
                P.dma(v_in[tt * 128:(tt + 1) * 128, c0:c0 + cw], s[:, 0:cw])
            linear_tm(w_in[l][:, C_VB:C_VB + D], RA, KC, D, ev_v)
            if stop_after == "a":
                return True

            kdE = RB[:, 0:8, :]
            kdO = RB[:, 8:16, :]
            va = RC.rearrange("p (a b) c -> p a (b c)", b=2)
            egb = RD[:, 0:8, :]

            def ev_al(n, th, pa, msz):
                P.copy(alT[0:16, th * 512:(th + 1) * 512], pa)
            linear_fm(w_in[l][:, C_AL:C_AL + 16], RA, KC, 16, ev_al)
            for tt in range(8):
                en = rdf(4 + tt % 2)
                spb = rdb(12 + tt % 2)
                for blk in range(2):
                    pb = ps[4 + blk]
                    P.matmul(pb[:], alT[0:16, tt * 128:(tt + 1) * 128], wgu_sb[0:16, blk * 512:(blk + 1) * 512],
                             start=True, stop=False)
                    P.matmul(pb[:], one_row, bg_sb[0:1, blk * 512:(blk + 1) * 512], start=False, stop=True)
                    P.activation(en[:, blk * 512:(blk + 1) * 512], pb[:], AF.Exp, scale=-1.0)
                    P.activation(spb[:, blk * 512:(blk + 1) * 512], en[:, blk * 512:(blk + 1) * 512], AF.Ln, bias=1.0)
                for blk in range(2):
                    gb = ps[4 + blk]
                    P.matmul(gb[:], m1_b, spb[:, blk * 512:(blk + 1) * 512])
                    P.activation(egb[:, tt, blk * 512:(blk + 1) * 512], gb[:], AF.Exp)
                for fc in range(8):
                    col = (tt * 8 + fc) * 2
                    P.matmul(ps[3][:, col:col + 2], spb[:, fc * 128:(fc + 1) * 128], csel_b)
            P.activation(dec[:], ps[3][:, 0:128], AF.Exp)

            def ev_ka(tt, c0, pa, cw):
                P.stt(kdE[:, tt, c0:c0 + cw], pa, rowE, egb[:, tt, c0:c0 + cw], ALU.mult, ALU.mult)
                P.stt(kdO[:, tt, c0:c0 + cw], pa, rowO, egb[:, tt, c0:c0 + cw], ALU.mult, ALU.mult)
                P.dma(kdE_d[tt * 128:(tt + 1) * 128, c0:c0 + cw], kdE[:, tt, c0:c0 + cw])
                P.dma(kdO_d[tt * 128:(tt + 1) * 128, c0:c0 + cw], kdO[:, tt, c0:c0 + cw])
            linear_tm(w_in[l][:, C_KA:C_KA + 1024], RA, KC, 1024, ev_ka)

            def ev_va(tt, c0, pa, cw):
                P.copy(va[:, tt, c0:c0 + cw], pa, eng=("act" if tt % 2 else "dve"))
                P.dma(va_d[tt * 128:(tt + 1) * 128, c0:c0 + cw], va[:, tt, c0:c0 + cw])
            linear_tm(w_in[l][:, C_VA:C_VA + D], RA, KC, D, ev_va)
            if stop_after == "b":
                return True

            qT4 = RD[:, 0:4, :]
            kvbuf = [[RD[:, 4 + 4 * s + i, :] for i in range(4)] for s in range(2)]
            RCf = RBC[:, 8:16, :].rearrange("p a (b c) -> p (a b) c", c=512)
            tset = [[RCf[:, 4 * s + i, :] for i in range(4)] for s in range(3)]
            wbf = [RCf[:, 12, :], RCf[:, 13, :], RDf[:, 14, :]]
            wT_sb = [RC[:, 14, :].rearrange("p (b q) -> p b q", b=2), RC[:, 15, :].rearrange("p (b q) -> p b q", b=2),
                     RD[:, 12, :].rearrange("p (b q) -> p b q", b=2), RD[:, 13, :].rearrange("p (b q) -> p b q", b=2)]
            tile_i = 0
            grp_i = 0
            SBV = int(os.environ.get("SBV", "6"))
            for hg in range(4):
                def ev_q(n, th, pa, msz):
                    P.copy(qT4[:, n % 4, th * 512:(th + 1) * 512], pa, eng=("act" if th else "dve"))
                linear_fm(w_in[l][:, C_QB + hg * 512:C_QB + (hg + 1) * 512], RA, KC, 512, ev_q)
                for hh in range(4):
                    h = hg * 4 + hh
                    kTo, kTp, vo_, vp_ = kvbuf[h % 2]
                    vo = vo_.rearrange("p (kb d) -> p kb d", d=128)
                    vp = vp_.rearrange("p (kb d) -> p kb d", d=128)
                    P.dma(kTo, kT_in[h * 128:(h + 1) * 128, :])
                    if half == 1:
                        P.dma(kTp, kT_dd[0][h * 128:(h + 1) * 128, :])
                    P.dma(vo, v_in[:, h * 128:(h + 1) * 128].rearrange("(kb p) d -> p kb d", p=128))
                    if half == 1:
                        P.dma(vp, v_dd[0][:, h * 128:(h + 1) * 128].rearrange("(kb p) d -> p kb d", p=128))
                    for qg in range(2):
                        chunks = [("o", qg, True)] + [("o", ci, False) for ci in reversed(range(qg))]
                        if half == 1:
                            chunks += [("p", 1, False), ("p", 0, False)]
                        oT = ps[3]
                        for cidx, (srcn, ci, diag) in enumerate(chunks):
                            kT = kTo if srcn == "o" else kTp
                            vv = vo if srcn == "o" else vp
                            wps = [ps[4], ps[5], ps[6], ps[7]]
                            wsb = [wT_sb[2 * (grp_i % 2)], wT_sb[2 * (grp_i % 2) + 1]]
                            grp_i += 1
                            for i in range(4):
                                qt = qg * 4 + i
                                wd = (i + 1) * 128 if diag else 512
                                e_, L_, C_, ec_ = tset[tile_i % 3]
                                wt = wbf[tile_i % 3]
                                zb = ps[tile_i % 3]
                                tile_i += 1
                                P.matmul(zb[:, 0:wd], qT4[:, hh, qt * 128:(qt + 1) * 128], kT[:, ci * 512:ci * 512 + wd])
                                P.activation(e_[:, 0:wd], zb[:, 0:wd], AF.Exp, scale=SCALE_SB)
                                P.activation(L_[:, 0:wd], e_[:, 0:wd], AF.Ln, bias=1.0)
                                if SBV < 2:
                                    continue
                                if diag:
                                    P.tt(L_[:, i * 128:(i + 1) * 128], L_[:, i * 128:(i + 1) * 128], maskf, ALU.mult)
                                    P.tt(e_[:, i * 128:(i + 1) * 128], e_[:, i * 128:(i + 1) * 128], maskf, ALU.mult)
                                P.scan(C_[:, 0:wd][:, ::-1], L_[:, 0:wd][:, ::-1],
                                       0.0 if diag else carry[:, qt:qt + 1])
                                P.copy(carry[:, qt:qt + 1], C_[:, 0:1])
                                if SBV < 3:
                                    continue
                                P.activation(ec_[:, 0:wd], C_[:, 0:wd], AF.Exp, scale=-1.0)
                                P.tt(wt[:, 0:wd], e_[:, 0:wd], ec_[:, 0:wd], ALU.mult)
                                for blk in range(wd // 128):
                                    P.transpose(wps[blk][:, i * 128:(i + 1) * 128],
                                                wt[:, blk * 128:(blk + 1) * 128], ident_f)
                            for blk in range(4):
                                if SBV < 4:
                                    continue
                                lo = blk * 128 if diag else 0
                                P.copy(wsb[blk // 2][:, blk % 2, lo:512], wps[blk][:, lo:512],
                                       eng=("act" if blk % 2 else "dve"))
                                if SBV >= 5:
                                    P.matmul(oT[:, lo:512], vv[:, ci * 4 + blk, :], wsb[blk // 2][:, blk % 2, lo:512],
                                             start=(cidx == 0 and blk == 0), stop=(cidx == len(chunks) - 1 and blk == 3))
                        if SBV >= 6:
                            P.copy(RB[:, h, qg * 512:(qg + 1) * 512], oT[:], eng="act")
            if l == 0:
                dump("sboT%d" % half, RB, BF16)
            if stop_after == "c":
                return True

            qE = RD[:, 0:2, :]
            qO = RD[:, 2:4, :]
            Sj = [RDf[:, 4, :], RDf[:, 5, :]]
            Sbf = [[RD[:, 12, 0:512], RD[:, 12, 512:1024]], [RD[:, 13, 0:512], RD[:, 13, 512:1024]]]
            strm = [(RD[:, 14, 0:256], RD[:, 14, 256:512], RD[:, 14, 512:1024]),
                    (RD[:, 15, 0:256], RD[:, 15, 256:512], RD[:, 15, 512:1024])]
            t1f = RDf[:, 8, :]
            rsb = RD[:, 6, 0:512]
            gbf = RDf[:, 9, :]
            ssum = smallf[:, 32:33]
            rs1 = smallf[:, 33:34]
            for qs in range(4):
                P.memset(RD[:, qs, :], 0.0, eng="pool")
            for h in range(4):
                for j in range(2):
                    if half == 0:
                        P.memset(Sj[j], 0.0, eng="pool")
                    else:
                        P.dma(Sj[j], st_d[(h * 2 + j) * 128:(h * 2 + j + 1) * 128, :])

                def ev_qa(n, th, pa, msz):
                    pv = pa.rearrange("p (a b c) -> p a b c", b=2, c=64)
                    qe = qE[:, n, th * 512:(th + 1) * 512].rearrange("p (a b c) -> p a b c", b=2, c=64)
                    qo = qO[:, n, th * 512:(th + 1) * 512].rearrange("p (a b c) -> p a b c", b=2, c=64)
                    P.activation(qe[:, :, 0, :], pv[:, :, 0, :], AF.Identity, scale=1.0 / 16.0)
                    P.activation(qo[:, :, 1, :], pv[:, :, 1, :], AF.Identity, scale=1.0 / 16.0)
                linear_fm(w_in[l][:, C_QA + h * 256:C_QA + (h + 1) * 256], RA, KC, 256, ev_qa, blk=256)
                wr = wload(w_in[l][:, C_RA + h * 512:C_RA + (h + 1) * 512], KC, 512)
                for tt in range(8):
                    kE_t, kO_t, va_t = strm[tt % 2]
                    P.dma(kE_t, kdE_d[tt * 128:(tt + 1) * 128, h * 256:(h + 1) * 256])
                    P.dma(kO_t, kdO_d[tt * 128:(tt + 1) * 128, h * 256:(h + 1) * 256])
                    P.dma(va_t, va_d[tt * 128:(tt + 1) * 128, h * 512:(h + 1) * 512])
                    rb = next_bank()
                    for kc in range(KC):
                        P.matmul(rb[:], RA[:, kc, tt * 128:(tt + 1) * 128], wr[:, kc, :], start=(kc == 0), stop=(kc == KC - 1))
                    P.activation(rsb, rb[:], AF.Silu)
                    for par in range(2):
                        kd_t = kE_t if par == 0 else kO_t
                        for j in range(2):
                            bank = ps[4 + j]
                            P.matmul(bank[:], kd_t[:, j * 128:(j + 1) * 128], va_t)
                            dcol = (tt * 8 + h * 2 + j) * 2 + par
                            P.stt(Sj[j], Sj[j], dec[:, dcol:dcol + 1], bank[:], ALU.mult, ALU.add)
                            P.copy(Sbf[par][j], Sj[j], eng="act")
                    ob = ps[3]
                    P.matmul(ob[:], qE[:, 0, tt * 128:(tt + 1) * 128], Sbf[0][0], start=True, stop=False)
                    P.matmul(ob[:], qE[:, 1, tt * 128:(tt + 1) * 128], Sbf[0][1], start=False, stop=False)
                    P.matmul(ob[:], qO[:, 0, tt * 128:(tt + 1) * 128], Sbf[1][0], start=False, stop=False)
                    P.matmul(ob[:], qO[:, 1, tt * 128:(tt + 1) * 128], Sbf[1][1], start=False, stop=True)
                    P.activation(t1f, ob[:], AF.Square)
                    P.scan(gbf, t1f, 0.0)
                    P.rsqrt(rs1, gbf[:, 511:512], EPS, 1.0 / 512.0)
                    P.stt(t1f, ob[:], rs1, gn_bc[:, h * 512:(h + 1) * 512], ALU.mult, ALU.mult)
                    P.tt(gbf, t1f, rsb, ALU.mult)
                    tb_ = ps[6]
                    for jj in range(4):
                        P.transpose(tb_[:, jj * 128:(jj + 1) * 128], gbf[:, jj * 128:(jj + 1) * 128], ident_f)
                    P.copy(RC[:, h * 4:(h + 1) * 4, tt * 128:(tt + 1) * 128],
                           tb_[:, 0:512].rearrange("p (a b) -> p a b", a=4), eng="act")
                if half == 0:
                    for j in range(2):
                        P.dma(st_d[(h * 2 + j) * 128:(h * 2 + j + 1) * 128, :], Sj[j])
            if l == 0:
                dump("glaoT%d" % half, RC, BF16)
            if stop_after == "d":
                return True

            tmpA = RDf[:, 0:4, :].rearrange("p (a b) c -> p a (b c)", b=2)
            tmpB = RDf[:, 4:8, :].rearrange("p (a b) c -> p a (b c)", b=2)
            for nb in range(8):
                def ev_ga(n, th, pa, msz):
                    P.activation(tmpA[:, n % 2, th * 512:(th + 1) * 512], pa, AF.Sigmoid)

                def ev_ya(n, th, pa, msz):
                    P.tt(tmpA[:, n % 2, th * 512:(th + 1) * 512], pa, tmpA[:, n % 2, th * 512:(th + 1) * 512], ALU.mult)

                def ev_gb(n, th, pa, msz):
                    P.activation(tmpB[:, n % 2, th * 512:(th + 1) * 512], pa, AF.Sigmoid)

                def ev_yb(n, th, pa, msz, nb=nb):
                    a = tmpA[:, n % 2, th * 512:(th + 1) * 512]
                    b_ = tmpB[:, n % 2, th * 512:(th + 1) * 512]
                    P.tt(b_, pa, b_, ALU.mult)
                    P.tt(a, a, b_, ALU.add)
                    mb = mixstg[:, (n * 2 + th) % 2, :]
                    P.copy(mb, a, eng="act")
                    P.dma(mixT_d[(nb * 2 + n % 2) * 128:(nb * 2 + n % 2 + 1) * 128, th * 512:(th + 1) * 512], mb)
                cs = slice(nb * 256, (nb + 1) * 256)
                linear_fm(w_in[l][:, C_GA + nb * 256:C_GA + (nb + 1) * 256], RA, KC, 256, ev_ga, blk=256)
                linear_fm(w_gla_o[l][:, cs], RC, KC, 256, ev_ya, blk=256)
                linear_fm(w_in[l][:, C_GB + nb * 256:C_GB + (nb + 1) * 256], RA, KC, 256, ev_gb, blk=256)
                linear_fm(w_sb_o[l][:, cs], RB, KC, 256, ev_yb, blk=256)
            if stop_after == "e":
                return True

            for kc in range(KC):
                P.dma(RA[:, kc, :], mixT_d[kc * 128:(kc + 1) * 128, :])
            if l == 0:
                dump("mixT%d" % half, RA[:], BF16)

            def ev_mo(n, th, pa, msz):
                P.copy(RBC[:, n, th * 512:(th + 1) * 512], pa, eng="dve")
                sumsq_accum(RBC[:, n, th * 512:(th + 1) * 512], n, th)
            linear_fm(w_out[l], RA, KC, D, ev_mo)
            resid_update(RBC, 32, xT_d)
            if stop_after == "f":
                return True

            norm_mod(48, 64, RA, xT_d)
            f1T = RD[:, 0:8, :]
            for fb in range(8):
                def ev_f1(n, th, pa, msz):
                    r_ = RDf[:, 8 + (n * 2 + th) % 2, :]
                    P.activation(r_, pa, AF.Relu)
                    P.tt(f1T[:, n % 8, th * 512:(th + 1) * 512], r_, r_, ALU.mult)
                linear_fm(w_ff1[l][:, fb * 1024:(fb + 1) * 1024], RA, KC, 1024, ev_f1)

                def ev_f2(n, th, pa, msz, fb=fb):
                    dst = RBC[:, n, th * 512:(th + 1) * 512]
                    if fb == 0:
                        P.copy(dst, pa, eng="act")
                    else:
                        P.tt(dst, dst, pa, ALU.add)
                    if fb == 7:
                        sumsq_accum(dst, n, th)
                linear_fm(w_ff2[l][fb * 1024:(fb + 1) * 1024, :], f1T, 8, D, ev_f2)
            resid_update(RBC, 80, xT_d)

            return False

        stop = False
        for l in range(depth):
            P.dma(gains[:], gainsT[l])
            P.dma(badT[:], b_adaT[l])
            P.dma(gn_bc[:], gn_bc_in[l])
            P.dma(wgu_sb[:], w_gu[l], q="pool")
            P.dma(bg_sb[:], b_gate[l], q="pool")
            pm = ps[3]
            for c0 in range(0, 6 * D, 512):
                wb = wload(w_ada[l][:, c0:c0 + 512], KC, 512)
                for j in range(4):
                    col = c0 // 128 + j
                    for kc in range(KC):
                        P.matmul(pm[:, col:col + 1], wb[:, kc, j * 128:(j + 1) * 128], cact[:, kc:kc + 1],
                                 start=(kc == 0), stop=(kc == KC - 1))
            P.tt(modT[:], pm[:, 0:96], badT[:], ALU.add)
            P.stt(coef[:, 0:16], modT[:, 16:32], 1.0, gains[:, 0:16], ALU.add, ALU.mult)
            P.copy(coef[:, 16:32], modT[:, 0:16])
            P.tt(coef[:, 32:48], modT[:, 32:48], gains[:, 16:32], ALU.mult)
            P.stt(coef[:, 48:64], modT[:, 64:80], 1.0, gains[:, 32:48], ALU.add, ALU.mult)
            P.copy(coef[:, 64:80], modT[:, 48:64])
            P.tt(coef[:, 80:96], modT[:, 80:96], gains[:, 48:64], ALU.mult)
            if l == 0:
                dump("modT", modT[:])

            for half in range(2):
                if layer_half(l, half):
                    stop = True
                    break
            if stop:
                break

        for tt16 in range(16):
            xT_d = xT_dd[tt16 // 8]
            tt = tt16 % 8
            for half in range(2):
                o = rdf((tt * 2 + half) % 2)
                for g in range(2):
                    kc0 = half * 8 + g * 4
                    src = rdf(2 + g)
                    P.dma(src[:, 0:512].rearrange("p (j t) -> p j t", j=4),
                          xT_d[kc0 * 128:(kc0 + 4) * 128, tt * 128:(tt + 1) * 128].rearrange("(j p) t -> p j t", p=128))
                    bank = next_bank()
                    for j in range(4):
                        P.transpose(bank[:, j * 128:(j + 1) * 128], src[:, j * 128:(j + 1) * 128], ident_f)
                    P.copy(o[:, g * 512:(g + 1) * 512], bank[:], eng=("act" if g else "dve"))
                P.dma(y_out[tt16 * 128:(tt16 + 1) * 128, half * 1024:(half + 1) * 1024], o, final=True)
        P.emit()
    return nc


def make_consts():
    c = np.zeros((128, 1024), np.float32)
    p = np.arange(128)[:, None]
    f = np.arange(128)[None, :]
    c[:, 0:128] = (p == f)
    c[:, 128:256] = (f < p)
    c[:, 256] = (np.arange(128) < 64)
    c[:, 257] = (np.arange(128) >= 64)
    c[:, 384:512] = 1.0 / 2048.0
    same = (p // 64) == (f // 64)
    c[:, 512:640] = np.where((p > f) & same, -1.0 / 16.0, 0.0)
    c[:, 640] = np.where(np.arange(128) < 64, -1.0 / 16.0, 0.0)
    c[:, 641] = np.where(np.arange(128) >= 64, -1.0 / 16.0, 0.0)
    c[:, 768:896] = 1.0
    return c


def make_in_maps(inputs, n_cores=8, depth=DEPTH):
    f32 = lambda a: np.ascontiguousarray(np.asarray(a, dtype=np.float32))
    x = f32(inputs["x"])
    c = f32(inputs["c"])
    L = DEPTH
    shared0 = {
        "w_ada": f32(inputs["w_ada"]),
        "w_in": f32(inputs["w_in"]),
        "w_gate_up": f32(inputs["w_gate_up"]),
        "w_gla_o": f32(inputs["w_gla_o"]),
        "w_sb_o": f32(inputs["w_sb_o"]),
        "w_out": f32(inputs["w_out"]),
        "w_ff1": f32(inputs["w_ff1"]),
        "w_ff2": f32(inputs["w_ff2"]),
        "b_adaT": np.ascontiguousarray(f32(inputs["b_ada"]).reshape(L, 96, 128).transpose(0, 2, 1)),
        "gainsT": np.ascontiguousarray(f32(inputs["norm_gains"]).reshape(L, 4, KC, 128).transpose(0, 3, 1, 2).reshape(L, 128, 4 * KC)),
        "b_gate": f32(inputs["b_gate"]).reshape(L, 1, 1024),
        "gn_bc": np.ascontiguousarray(np.broadcast_to(f32(inputs["gla_norm_gain"]).reshape(L, 1, 2048), (L, 128, 2048))),
    }
    shared = {k: np.ascontiguousarray(v[:depth]) for k, v in shared0.items()}
    shared["consts"] = make_consts()
    maps = []
    for core in range(n_cores):
        b = core % 4
        m = dict(shared)
        m["x"] = np.ascontiguousarray(x[b])
        m["cT"] = np.ascontiguousarray(c[b].reshape(KC, 128).T)
        maps.append(m)
    return maps


_NC_CACHE = {}
N_LAUNCH_CORES = 8


def kernel(**inputs):
    n = N_LAUNCH_CORES
    if "nc" not in _NC_CACHE:
        _NC_CACHE["nc"] = build_program(n)
    nc = _NC_CACHE["nc"]
    maps = make_in_maps(inputs, n)
    res = run_bass_kernel_spmd(nc, maps, core_ids=list(range(n)))
    out = np.empty((4, 2 * T, D), np.float32)
    for b in range(4):
        out[b] = np.asarray(res.results[b + 4]["y"], dtype=np.float32)
    if os.environ.get("KDIAG"):
        for b in range(4):
            a0 = np.asarray(res.results[b]["y"], dtype=np.float32)
            print("[kdiag] batch", b, "core", b, "vs core", b + 4, "maxabs diff", float(np.abs(a0 - out[b]).max()), flush=True)
    return out
```

```python
import os
import numpy as np
import ml_dtypes
from contextlib import ExitStack
import concourse.bass as bass
import concourse.mybir as mybir
from concourse.bass_utils import run_bass_kernel_spmd

F32 = mybir.dt.float32
BF16 = mybir.dt.bfloat16
AF = mybir.ActivationFunctionType
ALU = mybir.AluOpType

COMPUTE = ("pe", "act", "dve", "pool")
ALL_ENG = ("pe", "act", "dve", "pool", "sp")
N_DMA_SEMS = 24
N_CC_SEMS = 4
CC_INC = 1

D = 2048
T = 1024
KC = 16
DEPTH = 4
EPS = 1e-6
IN_COLS = 16400
C_QA, C_KA, C_VA, C_RA, C_AL, C_QB, C_KB, C_VB, C_GA, C_GB = (
    0, 1024, 2048, 4096, 6144, 6160, 8208, 10256, 12304, 14352)


class _Ins:
    __slots__ = ("eng", "fn", "waits", "signal", "ext", "idx", "clock")

    def __init__(self, eng, fn):
        self.eng = eng
        self.fn = fn
        self.waits = []
        self.signal = False
        self.ext = None
        self.idx = -1
        self.clock = None


def _ap_range(ap):
    t = ap.tensor
    shape = list(t.shape)
    rsz = 1
    for s in shape[1:]:
        rsz *= s
    esz = mybir.dt.size(ap.dtype)
    off = int(ap.offset)
    r_lo = r_hi = off // rsz
    c_lo = c_hi = off % rsz
    for step, cnt in ap.ap:
        step = int(step)
        cnt = int(cnt)
        if cnt <= 1 or step == 0:
            continue
        ext = step * (cnt - 1)
        if abs(step) >= rsz and step % rsz == 0:
            e = ext // rsz
            if e > 0:
                r_hi += e
            else:
                r_lo += e
        else:
            if ext > 0:
                c_hi += ext
            else:
                c_lo += ext
    return t.name, r_lo, r_hi + 1, c_lo * esz, (c_hi + 1) * esz


class Prog:
    def __init__(self, nc):
        self.nc = nc
        self.ins = {e: [] for e in ALL_ENG}
        self.regs = {}
        self.known = {e: {} for e in ALL_ENG}
        self.known_ext = {e: set() for e in ALL_ENG}
        self.exts = []
        self.slot_last = {"d": [None] * N_DMA_SEMS, "c": [None] * N_CC_SEMS}
        self.slot_cnt = {"d": [0] * N_DMA_SEMS, "c": [0] * N_CC_SEMS}
        self.next_slot = {"d": 0, "c": 0}
        self.out_exts = []

    def _add_dep(self, ins, dep, raw):
        if dep is None:
            return
        eng = ins.eng
        if dep[0] == "x":
            if dep[1] in self.known_ext[eng]:
                return
            ins.waits.append(dep)
            return
        _, e2, idx = dep
        if e2 == eng:
            if not raw or eng == "pe":
                return
            if ins.idx - idx > 2:
                return
            ins.waits.append(dep)
            return
        if self.known[eng].get(e2, -1) >= idx:
            return
        ins.waits.append(dep)

    def add(self, eng, fn, reads=(), writes=(), ext=None, out=False):
        ins = _Ins(eng, fn)
        ins.idx = len(self.ins[eng])
        me = ("e", eng, ins.idx)
        prev_wait = None
        if ext is not None:
            kind = ext
            n = N_DMA_SEMS if kind == "d" else N_CC_SEMS
            xid = len(self.exts)
            slot = self.next_slot[kind]
            self.next_slot[kind] = (slot + 1) % n
            prev = self.slot_last[kind][slot]
            if prev is not None and prev not in self.known_ext[eng]:
                prev_wait = ("x", prev)
            self.slot_cnt[kind][slot] += 16 if kind == "d" else CC_INC
            self.exts.append((kind, slot, self.slot_cnt[kind][slot]))
            self.slot_last[kind][slot] = xid
            ins.ext = xid
            me = ("x", xid)
            if out:
                self.out_exts.append(xid)
        rr = [_ap_range(a) for a in reads]
        wr = [_ap_range(a) for a in writes]
        for (nm, r0, r1, c0, c1) in rr:
            for ent in self.regs.get(nm, ()):
                if ent[0] < r1 and r0 < ent[1] and ent[2] < c1 and c0 < ent[3]:
                    self._add_dep(ins, ent[4], True)
        for (nm, r0, r1, c0, c1) in wr:
            for ent in self.regs.get(nm, ()):
                if ent[0] < r1 and r0 < ent[1] and ent[2] < c1 and c0 < ent[3]:
                    self._add_dep(ins, ent[4], False)
                    for rk, rv in ent[5].items():
                        if isinstance(rk, tuple):
                            self._add_dep(ins, rk, False)
                        else:
                            self._add_dep(ins, ("e", rk, rv), False)
        best = {}
        xs = []
        for w in ins.waits:
            if w[0] == "e":
                if best.get(w[1], -1) < w[2]:
                    best[w[1]] = w[2]
            elif w not in xs:
                xs.append(w)
        if prev_wait is not None and prev_wait not in xs:
            xs.append(prev_wait)
        ins.waits = [("e", e, i) for e, i in best.items()] + xs
        kn = self.known[eng]
        kx = self.known_ext[eng]
        for w in ins.waits:
            if w[0] == "e":
                _, e2, i2 = w
                src = self.ins[e2][i2]
                src.signal = True
                if e2 != eng:
                    if kn.get(e2, -1) < i2:
                        kn[e2] = i2
                    ck = src.clock
                    for e3, i3 in ck[0].items():
                        if e3 != eng and kn.get(e3, -1) < i3:
                            kn[e3] = i3
                    kx |= ck[1]
            else:
                kx.add(w[1])
        ins.clock = (dict(kn), set(kx))
        for (nm, r0, r1, c0, c1) in rr:
            lst = self.regs.setdefault(nm, [])
            hit = None
            for ent in lst:
                if ent[0] <= r0 and r1 <= ent[1] and ent[2] <= c0 and c1 <= ent[3]:
                    hit = ent
                    break
            if hit is None:
                hit = [r0, r1, c0, c1, None, {}]
                lst.append(hit)
            if me[0] == "x":
                hit[5][me] = True
            else:
                hit[5][eng] = ins.idx
        for (nm, r0, r1, c0, c1) in wr:
            lst = self.regs.setdefault(nm, [])
            keep = [ent for ent in lst
                    if not (r0 <= ent[0] and ent[1] <= r1 and c0 <= ent[2] and ent[3] <= c1)]
            keep.append([r0, r1, c0, c1, me, {}])
            self.regs[nm] = keep
        self.ins[eng].append(ins)
        return ins

    def matmul(self, out, lhsT, rhs, start=True, stop=True):
        return self.add("pe", lambda e: e.matmul(out, lhsT, rhs, start=start, stop=stop),
                        reads=[lhsT, rhs], writes=[out])

    def transpose(self, out, in_, ident):
        return self.add("pe", lambda e: e.transpose(out, in_, ident), reads=[in_, ident], writes=[out])

    def activation(self, out, in_, func, bias=None, scale=None, accum_out=None, eng="act"):
        kw = {}
        rd = [in_]
        wr = [out]
        if bias is not None:
            kw["bias"] = bias
            if not isinstance(bias, (int, float)):
                rd.append(bias)
        if scale is not None:
            kw["scale"] = scale
            if not isinstance(scale, (int, float)):
                rd.append(scale)
        if accum_out is not None:
            kw["accum_out"] = accum_out
            wr.append(accum_out)
        return self.add(eng, lambda e: e.activation(out=out, in_=in_, func=func, **kw), reads=rd, writes=wr)

    def tt(self, out, in0, in1, op, eng="dve"):
        return self.add(eng, lambda e: e.tensor_tensor(out=out, in0=in0, in1=in1, op=op),
                        reads=[in0, in1], writes=[out])

    def ts(self, out, in0, s1, s2, op0, op1=None, eng="dve"):
        rd = [in0] + [s for s in (s1, s2) if s is not None and not isinstance(s, (int, float))]
        if op1 is None:
            return self.add(eng, lambda e: e.tensor_scalar(out=out, in0=in0, scalar1=s1, scalar2=None, op0=op0),
                            reads=rd, writes=[out])
        return self.add(eng, lambda e: e.tensor_scalar(out=out, in0=in0, scalar1=s1, scalar2=s2, op0=op0, op1=op1),
                        reads=rd, writes=[out])

    def stt(self, out, in0, scalar, in1, op0, op1, eng="dve"):
        rd = [in0, in1] + ([] if isinstance(scalar, (int, float)) else [scalar])
        return self.add(eng, lambda e: e.scalar_tensor_tensor(out=out, in0=in0, scalar=scalar, in1=in1, op0=op0, op1=op1),
                        reads=rd, writes=[out])

    def copy(self, out, in_, eng="dve"):
        if eng == "act":
            return self.activation(out, in_, AF.Copy)
        return self.add(eng, lambda e: e.tensor_copy(out=out, in_=in_), reads=[in_], writes=[out])

    def scan(self, out, data, initial, eng="dve"):
        rd = [data] + ([] if isinstance(initial, (int, float)) else [initial])
        return self.add(eng, lambda e: e.tensor_tensor_scan(out=out, data0=data, data1=data, initial=initial,
                                                          op0=ALU.add, op1=ALU.bypass), reads=rd, writes=[out])

    def rsqrt(self, out, in_, eps, scale_in=1.0):
        self.activation(out, in_, AF.Ln, bias=eps, scale=scale_in)
        return self.activation(out, out, AF.Exp, scale=-0.5)

    def memset(self, out, val, eng="dve"):
        return self.add(eng, lambda e: e.memset(out, val), writes=[out])

    def dma(self, out, in_, q="sp", final=False):
        return self.add(q, lambda e: e.dma_start(out=out, in_=in_), reads=[in_], writes=[out], ext="d", out=final)

    def allgather(self, out, in_, groups):
        return self.add("pool", lambda e: e.collective_compute("AllGather", ALU.bypass, replica_groups=groups,
                                                               ins=[in_], outs=[out]),
                        reads=[in_], writes=[out], ext="c")

    def emit(self):
        nc = self.nc
        with ExitStack() as st:
            esem = {e: st.enter_context(nc.semaphore("s_" + e)) for e in COMPUTE}
            xsem = {"d": [st.enter_context(nc.semaphore("d_%d" % i)) for i in range(N_DMA_SEMS)],
                    "c": [st.enter_context(nc.semaphore("c_%d" % i)) for i in range(N_CC_SEMS)]}
            cnt = {}
            for e in COMPUTE:
                c = 0
                arr = []
                for ins in self.ins[e]:
                    if ins.signal:
                        c += 1
                    arr.append(c)
                cnt[e] = arr

            def run(ename, h):
                for ins in self.ins[ename]:
                    for w in ins.waits:
                        if w[0] == "e":
                            h.wait_ge(esem[w[1]], cnt[w[1]][w[2]])
                        else:
                            kind, slot, val = self.exts[w[1]]
                            h.wait_ge(xsem[kind][slot], val)
                    bi = ins.fn(h)
                    if ins.ext is not None:
                        kind, slot, val = self.exts[ins.ext]
                        bi.then_inc(xsem[kind][slot], 16 if kind == "d" else CC_INC)
                    elif ins.signal:
                        bi.then_inc(esem[ename], 1)
                if ename == "sp":
                    for xid in self.out_exts:
                        kind, slot, val = self.exts[xid]
                        h.wait_ge(xsem[kind][slot], val)

            with nc.Block() as block:
                @block.tensor
                def _(h):
                    run("pe", h)

                @block.scalar
                def _(h):
                    run("act", h)

                @block.vector
                def _(h):
                    run("dve", h)

                @block.gpsimd
                def _(h):
                    run("pool", h)

                @block.sync
                def _(h):
                    run("sp", h)


_DSZ = {}


def build_program(n_cores=8, depth=DEPTH, dbg=None, stop_after=None):
    dbg = dbg or set()
    nc = bass.Bass("TRN2", target_bir_lowering=False)
    groups = [[2 * i, 2 * i + 1] for i in range(n_cores // 2)]
    P = Prog(nc)

    def din(name, shape, dt=F32):
        return nc.dram_tensor(name, list(shape), dt, kind="ExternalInput").ap()

    def dout(name, shape, dt=F32):
        return nc.dram_tensor(name, list(shape), dt, kind="ExternalOutput").ap()

    def dscr(name, shape, dt):
        return nc.dram_tensor(name, list(shape), dt).ap()

    x_in = din("x", [2 * T, D])
    cT_in = din("cT", [128, KC])
    w_ada = din("w_ada", [depth, D, 6 * D])
    b_adaT = din("b_adaT", [depth, 128, 96])
    gainsT = din("gainsT", [depth, 128, 4 * KC])
    w_in = din("w_in", [depth, D, IN_COLS])
    w_gu = din("w_gate_up", [depth, 16, 1024])
    b_gate = din("b_gate", [depth, 1, 1024])
    gn_bc_in = din("gn_bc", [depth, 128, 2048])
    w_gla_o = din("w_gla_o", [depth, D, D])
    w_sb_o = din("w_sb_o", [depth, D, D])
    w_out = din("w_out", [depth, D, D])
    w_ff1 = din("w_ff1", [depth, D, 4 * D])
    w_ff2 = din("w_ff2", [depth, 4 * D, D])
    consts_in = din("consts", [128, 1024])
    y_out = dout("y", [2 * T, D])

    xT_dd = [dscr("xT_d%d" % i, [D, T], F32) for i in range(2)]
    kT_dd = [dscr("kT_d%d" % i, [D, T], BF16) for i in range(2)]
    v_dd = [dscr("v_d%d" % i, [T, D], BF16) for i in range(2)]
    st_d = dscr("st_d", [1024, 512], F32)
    kdE_d = dscr("kdE_d", [T, 1024], BF16)
    kdO_d = dscr("kdO_d", [T, 1024], BF16)
    va_d = dscr("va_d", [T, D], BF16)
    mixT_d = dscr("mixT_d", [D, T], BF16)

    SCALE_SB = 128.0 ** -0.5

    with ExitStack() as st:
        def sb(name, shape, dt):
            return st.enter_context(nc.sbuf_tensor(name, list(shape), dt))

        ps = [st.enter_context(nc.psum_tensor("ps%d" % i, [128, 512], F32)) for i in range(8)]

        cst = sb("cst", [128, 1024], F32)
        ident_f = cst[:, 0:128]
        maskf = cst[:, 128:256]
        rowE = cst[:, 256:257]
        rowO = cst[:, 257:258]
        cstb = sb("cstb", [128, 768], BF16)
        ident_b = cstb[:, 0:128]
        ones_b = cstb[:, 128:256]
        m1_b = cstb[:, 256:384]
        csel_b = cstb[:, 384:386]
        one_row = cstb[0:1, 512:640]
        cact = sb("cact", [128, KC], BF16)
        modT = sb("modT", [128, 96], F32)
        badT = sb("badT", [128, 96], F32)
        gains = sb("gains", [128, 4 * KC], F32)
        coef = sb("coef", [128, 6 * KC], F32)
        smallf = sb("smallf", [128, 64], F32)
        dec = sb("dec", [128, 128], F32)
        carry = sb("carry", [128, 8], F32)
        alT = sb("alT", [16, 1024], BF16)
        wgu_sb = sb("wgu_sb", [16, 1024], BF16)
        bg_sb = sb("bg_sb", [1, 1024], BF16)
        gn_bc = sb("gn_bc_sb", [128, 2048], F32)
        rstd_bc = sb("rstd_bc", [128, 2, 512], F32)
        mixstg = sb("mixstg", [128, 2, 512], BF16)
        wbuf = [sb("wbuf%d" % i, [128, KC, 512], BF16) for i in range(3)]
        RA = sb("RA", [128, KC, T], BF16)
        RBC = sb("RBC", [128, KC, T], F32)
        RD = sb("RD", [128, KC, T], BF16)
        RB = RBC[:, 0:8, :].bitcast(BF16).rearrange("p a (b c) -> p (a b) c", c=T)
        RC = RBC[:, 8:16, :].bitcast(BF16).rearrange("p a (b c) -> p (a b) c", c=T)
        RDf = RD[:].bitcast(F32)

        def rdf(slot):
            return RDf[:, 2 * slot:2 * slot + 2, :].rearrange("p a b -> p (a b)")

        def rdb(slot):
            return RD[:, slot, :]

        st_w = {"i": 0}

        def wload(src2d, kc_n, ncols):
            wb = wbuf[st_w["i"] % 3]
            st_w["i"] += 1
            srcv = src2d.rearrange("(kc p) n -> p kc n", p=128)
            step = 4 if ncols >= 256 else kc_n
            for k0 in range(0, kc_n, step):
                k1 = min(kc_n, k0 + step)
                P.dma(wb[:, k0:k1, 0:ncols], srcv[:, k0:k1, :], q="pool")
            return wb

        lin_bank = {"i": 0}

        def next_bank():
            b = ps[lin_bank["i"] % 3]
            lin_bank["i"] += 1
            return b

        def linear_fm(wsrc, in_sb, kc_n, ncols, evac, blk=512):
            for c0 in range(0, ncols, blk):
                cw = min(blk, ncols - c0)
                wb = wload(wsrc[:, c0:c0 + cw], kc_n, cw)
                for j0 in range(0, cw, 128):
                    msz = min(128, cw - j0)
                    for th in range(2):
                        bank = next_bank()
                        for kc in range(kc_n):
                            P.matmul(bank[0:msz, :], wb[:, kc, j0:j0 + msz], in_sb[:, kc, th * 512:(th + 1) * 512],
                                     start=(kc == 0), stop=(kc == kc_n - 1))
                        evac((c0 + j0) // 128, th, bank[0:msz, :], msz)

        def linear_tm(wsrc, in_sb, kc_n, ncols, evac):
            for c0 in range(0, ncols, 512):
                cw = min(512, ncols - c0)
                wb = wload(wsrc[:, c0:c0 + cw], kc_n, cw)
                for tt in range(8):
                    bank = next_bank()
                    for kc in range(kc_n):
                        P.matmul(bank[:, 0:cw], in_sb[:, kc, tt * 128:(tt + 1) * 128], wb[:, kc, 0:cw],
                                 start=(kc == 0), stop=(kc == kc_n - 1))
                    evac(tt, c0, bank[:, 0:cw], cw)

        def dump(name, ap_src, dt=F32):
            if name not in dbg:
                return
            o = dout("dbg_" + name, list(ap_src.shape), dt)
            P.dma(o, ap_src, final=True)

        P.dma(cst[:], consts_in)
        P.copy(cstb[:, 0:128], cst[:, 0:128])
        P.copy(cstb[:, 128:768], cst[:, 384:1024])
        P.dma(smallf[:, 0:KC], cT_in)
        P.activation(cact[:], smallf[:, 0:KC], AF.Silu)
        for tt16 in range(16):
            xT_d = xT_dd[tt16 // 8]
            tt = tt16 % 8
            xt = rdf(tt % 2)
            xt2 = rdf(2 + tt % 2)
            P.dma(xt, x_in[tt16 * 128:(tt16 + 1) * 128, 0:1024])
            P.dma(xt2, x_in[tt16 * 128:(tt16 + 1) * 128, 1024:2048])
            for half, src in enumerate((xt, xt2)):
                for g in range(2):
                    bank = next_bank()
                    for j in range(4):
                        P.transpose(bank[:, j * 128:(j + 1) * 128], src[:, (g * 4 + j) * 128:(g * 4 + j + 1) * 128], ident_f)
                    o = rdf(4 + (half * 2 + g) % 2)
                    P.copy(o[:, 0:512], bank[:], eng=("act" if g else "dve"))
                    kc0 = half * 8 + g * 4
                    P.dma(xT_d[kc0 * 128:(kc0 + 4) * 128, tt * 128:(tt + 1) * 128].rearrange("(j p) t -> p j t", p=128),
                          o[:, 0:512].rearrange("p (j t) -> p j t", j=4))

        def sumsq_bank(th):
            return ps[4 + th]

        def norm_mod(a_col, sh_col, dst, xT_d):
            for kc in range(KC):
                xs = rdf(kc % 3)
                P.dma(xs, xT_d[kc * 128:(kc + 1) * 128, :])
                for th in range(2):
                    sq = rdb(12 + (kc * 2 + th) % 4)
                    P.activation(sq[:, 0:512], xs[:, th * 512:(th + 1) * 512], AF.Square)
                    P.matmul(sumsq_bank(th)[:], ones_b, sq[:, 0:512], start=(kc == 0), stop=(kc == KC - 1))
            for th in range(2):
                P.rsqrt(rstd_bc[:, th, :], sumsq_bank(th)[:], EPS)
            for kc in range(KC):
                xs = rdf(kc % 3)
                P.dma(xs, xT_d[kc * 128:(kc + 1) * 128, :])
                t1 = rdf(3 + kc % 2)
                for th in range(2):
                    P.stt(t1[:, th * 512:(th + 1) * 512], xs[:, th * 512:(th + 1) * 512],
                          coef[:, a_col + kc:a_col + kc + 1], rstd_bc[:, th, :], ALU.mult, ALU.mult)
                P.activation(dst[:, kc, :], t1, AF.Identity, bias=coef[:, sh_col + kc:sh_col + kc + 1])

        def sumsq_accum(src_f32, n, th):
            sq = rdb(12 + (n * 2 + th) % 4)
            P.activation(sq[:, 0:512], src_f32, AF.Square)
            P.matmul(sumsq_bank(th)[:], ones_b, sq[:, 0:512], start=(n == 0), stop=(n == KC - 1))

        def resid_update(yT, g_col, xT_d):
            for th in range(2):
                P.rsqrt(rstd_bc[:, th, :], sumsq_bank(th)[:], EPS)
            for kc in range(KC):
                xs = rdf(kc % 3)
                P.dma(xs, xT_d[kc * 128:(kc + 1) * 128, :])
                t1 = rdf(3 + kc % 2)
                for th in range(2):
                    P.stt(t1[:, th * 512:(th + 1) * 512], yT[:, kc, th * 512:(th + 1) * 512],
                          coef[:, g_col + kc:g_col + kc + 1], rstd_bc[:, th, :], ALU.mult, ALU.mult)
                P.tt(xs, xs, t1, ALU.add)
                P.dma(xT_d[kc * 128:(kc + 1) * 128, :], xs)

        def layer_half(l, half):
            xT_d = xT_dd[half]
            kT_in = kT_dd[half]
            v_in = v_dd[half]
            norm_mod(0, 16, RA, xT_d)
            if l == 0:
                dump("hT%d" % half, RA[:], BF16)
            if stop_after == "h":
                return True

            def ev_k(n, th, pa, msz):
                s = rdb(10 + n % 2)
                P.copy(s[:, th * 512:(th + 1) * 512], pa, eng=("act" if th else "dve"))
                if th == 1:
                    P.dma(kT_in[n * 128:(n + 1) * 128, :], s)
            linear_fm(w_in[l][:, C_KB:C_KB + D], RA, KC, D, ev_k)

            vi = {"i": 0}

            def ev_v(tt, c0, pa, cw):
                s = rdb(10 + vi["i"] % 4)
                vi["i"] += 1
                P.copy(s[:, 0:cw], pa, eng=("act" if tt % 2 else "dve"))
                P.dma(v_in[tt * 128:(tt + 1) * 128, c0:c0 + cw], s[:, 0:cw])
            linear_tm(w_in[l][:, C_VB:C_VB + D], RA, KC, D, ev_v)
            if stop_after == "a":
                return True

            kdE = RB[:, 0:8, :]
            kdO = RB[:, 8:16, :]
            va = RC.rearrange("p (a b) c -> p a (b c)", b=2)
            egb = RD[:, 0:8, :]

            def ev_al(n, th, pa, msz):
                P.copy(alT[0:16, th * 512:(th + 1) * 512], pa)
            linear_fm(w_in[l][:, C_AL:C_AL + 16], RA, KC, 16, ev_al)
            for tt in range(8):
                en = rdf(4 + tt % 2)
                spb = rdb(12 + tt % 2)
                for blk in range(2):
                    pb = ps[4 + blk]
                    P.matmul(pb[:], alT[0:16, tt * 128:(tt + 1) * 128], wgu_sb[0:16, blk * 512:(blk + 1) * 512],
                             start=True, stop=False)
                    P.matmul(pb[:], one_row, bg_sb[0:1, blk * 512:(blk + 1) * 512], start=False, stop=True)
                    P.activation(en[:, blk * 512:(blk + 1) * 512], pb[:], AF.Exp, scale=-1.0)
                    P.activation(spb[:, blk * 512:(blk + 1) * 512], en[:, blk * 512:(blk + 1) * 512], AF.Ln, bias=1.0)
                for blk in range(2):
                    gb = ps[4 + blk]
                    P.matmul(gb[:], m1_b, spb[:, blk * 512:(blk + 1) * 512])
                    P.activation(egb[:, tt, blk * 512:(blk + 1) * 512], gb[:], AF.Exp)
                for fc in range(8):
                    col = (tt * 8 + fc) * 2
                    P.matmul(ps[3][:, col:col + 2], spb[:, fc * 128:(fc + 1) * 128], csel_b)
            P.activation(dec[:], ps[3][:, 0:128], AF.Exp)

            def ev_ka(tt, c0, pa, cw):
                P.stt(kdE[:, tt, c0:c0 + cw], pa, rowE, egb[:, tt, c0:c0 + cw], ALU.mult, ALU.mult)
                P.stt(kdO[:, tt, c0:c0 + cw], pa, rowO, egb[:, tt, c0:c0 + cw], ALU.mult, ALU.mult)
                P.dma(kdE_d[tt * 128:(tt + 1) * 128, c0:c0 + cw], kdE[:, tt, c0:c0 + cw])
                P.dma(kdO_d[tt * 128:(tt + 1) * 128, c0:c0 + cw], kdO[:, tt, c0:c0 + cw])
            linear_tm(w_in[l][:, C_KA:C_KA + 1024], RA, KC, 1024, ev_ka)

            def ev_va(tt, c0, pa, cw):
                P.copy(va[:, tt, c0:c0 + cw], pa, eng=("act" if tt % 2 else "dve"))
                P.dma(va_d[tt * 128:(tt + 1) * 128, c0:c0 + cw], va[:, tt, c0:c0 + cw])
            linear_tm(w_in[l][:, C_VA:C_VA + D], RA, KC, D, ev_va)
            if stop_after == "b":
                return True

            qT4 = RD[:, 0:4, :]
            kvbuf = [[RD[:, 4 + 4 * s + i, :] for i in range(4)] for s in range(2)]
            RCf = RBC[:, 8:16, :].rearrange("p a (b c) -> p (a b) c", c=512)
            tset = [[RCf[:, 4 * s + i, :] for i in range(4)] for s in range(3)]
            wbf = [RCf[:, 12, :], RCf[:, 13, :], RDf[:, 14, :]]
            wT_sb = [RC[:, 14, :].rearrange("p (b q) -> p b q", b=2), RC[:, 15, :].rearrange("p (b q) -> p b q", b=2),
                     RD[:, 12, :].rearrange("p (b q) -> p b q", b=2), RD[:, 13, :].rearrange("p (b q) -> p b q", b=2)]
            tile_i = 0
            grp_i = 0
            for hg in range(4):
                def ev_q(n, th, pa, msz):
                    P.copy(qT4[:, n % 4, th * 512:(th + 1) * 512], pa, eng=("act" if th else "dve"))
                linear_fm(w_in[l][:, C_QB + hg * 512:C_QB + (hg + 1) * 512], RA, KC, 512, ev_q)
                tiles = []
                for hh in range(4):
                    h = hg * 4 + hh
                    for qg in range(2):
                        chunks = [("o", qg, True)] + [("o", ci, False) for ci in reversed(range(qg))]
                        if half == 1:
                            chunks += [("p", 1, False), ("p", 0, False)]
                        for cidx, (srcn, ci, diag) in enumerate(chunks):
                            g = dict(h=h, hh=hh, qg=qg, cidx=cidx, nch=len(chunks), srcn=srcn, ci=ci, diag=diag, gi=grp_i)
                            grp_i += 1
                            for i in range(4):
                                tiles.append(dict(g=g, i=i, t=tile_i, first_of_head=(qg == 0 and cidx == 0 and i == 0),
                                                  last_of_group=(i == 3)))
                                tile_i += 1

                def kv_of(h):
                    kTo, kTp, vo_, vp_ = kvbuf[h % 2]
                    return (kTo, kTp, vo_.rearrange("p (kb d) -> p kb d", d=128), vp_.rearrange("p (kb d) -> p kb d", d=128))

                NSET = 6

                def bufs(t):
                    return RCf[:, 2 * (t % NSET), :], RCf[:, 2 * (t % NSET) + 1, :]

                def info(tl):
                    g, i = tl["g"], tl["i"]
                    return g, i, tl["t"], ((i + 1) * 128 if g["diag"] else 512), g["qg"] * 4 + i

                def sA1(tl):
                    g, i, t, wd, qt = info(tl)
                    h, hh, ci = g["h"], g["hh"], g["ci"]
                    kTo, kTp, vo, vp = kv_of(h)
                    if tl["first_of_head"]:
                        P.dma(kTo, kT_in[h * 128:(h + 1) * 128, :])
                        P.dma(vo, v_in[:, h * 128:(h + 1) * 128].rearrange("(kb p) d -> p kb d", p=128))
                        if half == 1:
                            P.dma(kTp, kT_dd[0][h * 128:(h + 1) * 128, :])
                            P.dma(vp, v_dd[0][:, h * 128:(h + 1) * 128].rearrange("(kb p) d -> p kb d", p=128))
                    kT = kTo if g["srcn"] == "o" else kTp
                    E, L = bufs(t)
                    zb = ps[t % 3]
                    P.matmul(zb[:, 0:wd], qT4[:, hh, qt * 128:(qt + 1) * 128], kT[:, ci * 512:ci * 512 + wd])
                    P.activation(E[:, 0:wd], zb[:, 0:wd], AF.Exp, scale=SCALE_SB)

                def sA2(tl):
                    g, i, t, wd, qt = info(tl)
                    E, L = bufs(t)
                    P.activation(L[:, 0:wd], E[:, 0:wd], AF.Ln, bias=1.0)

                def sA3(tl):
                    g, i, t, wd, qt = info(tl)
                    E, L = bufs(t)
                    if g["diag"]:
                        P.tt(L[:, i * 128:(i + 1) * 128], L[:, i * 128:(i + 1) * 128], maskf, ALU.mult)
                        P.tt(E[:, i * 128:(i + 1) * 128], E[:, i * 128:(i + 1) * 128], maskf, ALU.mult)
                    P.scan(L[:, 0:wd][:, ::-1], L[:, 0:wd][:, ::-1], 0.0 if g["diag"] else carry[:, qt:qt + 1])

                def sCarry(tl):
                    g, i, t, wd, qt = info(tl)
                    E, L = bufs(t)
                    P.copy(carry[:, qt:qt + 1], L[:, 0:1])

                def sB1(tl):
                    g, i, t, wd, qt = info(tl)
                    E, L = bufs(t)
                    P.activation(L[:, 0:wd], L[:, 0:wd], AF.Exp, scale=-1.0)

                def sB2(tl):
                    g, i, t, wd, qt = info(tl)
                    E, L = bufs(t)
                    P.tt(E[:, 0:wd], E[:, 0:wd], L[:, 0:wd], ALU.mult)
                    for blk in range(wd // 128):
                        P.transpose(ps[4 + blk][:, i * 128:(i + 1) * 128], E[:, blk * 128:(blk + 1) * 128], ident_f)

                def stageC(g):
                    h, qg, ci, diag = g["h"], g["qg"], g["ci"], g["diag"]
                    kTo, kTp, vo, vp = kv_of(h)
                    vv = vo if g["srcn"] == "o" else vp
                    wsb = [wT_sb[2 * (g["gi"] % 2)], wT_sb[2 * (g["gi"] % 2) + 1]]
                    oT = ps[3]
                    for blk in range(4):
                        lo = blk * 128 if diag else 0
                        P.copy(wsb[blk // 2][:, blk % 2, lo:512], ps[4 + blk][:, lo:512], eng=("act" if blk % 2 else "dve"))
                        P.matmul(oT[:, lo:512], vv[:, ci * 4 + blk, :], wsb[blk // 2][:, blk % 2, lo:512],
                                 start=(g["cidx"] == 0 and blk == 0), stop=(g["cidx"] == g["nch"] - 1 and blk == 3))
                    if g["cidx"] == g["nch"] - 1:
                        P.copy(RB[:, h, qg * 512:(qg + 1) * 512], oT[:], eng="act")

                nt = len(tiles)
                for k in range(nt + 4):
                    if k < nt:
                        sA1(tiles[k])
                    if 0 <= k - 1 < nt:
                        sA2(tiles[k - 1])
                    if 0 <= k - 2 < nt:
                        sA3(tiles[k - 2])
                    if 0 <= k - 3 < nt:
                        sCarry(tiles[k - 3])
                        sB1(tiles[k - 3])
                    if 0 <= k - 4 < nt:
                        tl = tiles[k - 4]
                        sB2(tl)
                        if tl["last_of_group"]:
                            stageC(tl["g"])
            if l == 0:
                dump("sboT%d" % half, RB, BF16)
            if stop_after == "c":
                return True

            qE = RD[:, 0:2, :]
            qO = RD[:, 2:4, :]
            Sj = [RDf[:, 4, :], RDf[:, 5, :]]
            Sbf = [[RD[:, 12, 0:512], RD[:, 12, 512:1024]], [RD[:, 13, 0:512], RD[:, 13, 512:1024]]]
            strm = [(RD[:, 14, 0:256], RD[:, 14, 256:512], RD[:, 14, 512:1024]),
                    (RD[:, 15, 0:256], RD[:, 15, 256:512], RD[:, 15, 512:1024])]
            t1f = RDf[:, 8, :]
            rsb = RD[:, 6, 0:512]
            gbf = RDf[:, 9, :]
            ssum = smallf[:, 32:33]
            rs1 = smallf[:, 33:34]
            for qs in range(4):
                P.memset(RD[:, qs, :], 0.0, eng="pool")
            for h in range(4):
                for j in range(2):
                    if half == 0:
                        P.memset(Sj[j], 0.0, eng="pool")
                    else:
                        P.dma(Sj[j], st_d[(h * 2 + j) * 128:(h * 2 + j + 1) * 128, :])

                def ev_qa(n, th, pa, msz):
                    pv = pa.rearrange("p (a b c) -> p a b c", b=2, c=64)
                    qe = qE[:, n, th * 512:(th + 1) * 512].rearrange("p (a b c) -> p a b c", b=2, c=64)
                    qo = qO[:, n, th * 512:(th + 1) * 512].rearrange("p (a b c) -> p a b c", b=2, c=64)
                    P.activation(qe[:, :, 0, :], pv[:, :, 0, :], AF.Identity, scale=1.0 / 16.0)
                    P.activation(qo[:, :, 1, :], pv[:, :, 1, :], AF.Identity, scale=1.0 / 16.0)
                linear_fm(w_in[l][:, C_QA + h * 256:C_QA + (h + 1) * 256], RA, KC, 256, ev_qa, blk=256)
                wr = wload(w_in[l][:, C_RA + h * 512:C_RA + (h + 1) * 512], KC, 512)
                for tt in range(8):
                    kE_t, kO_t, va_t = strm[tt % 2]
                    P.dma(kE_t, kdE_d[tt * 128:(tt + 1) * 128, h * 256:(h + 1) * 256])
                    P.dma(kO_t, kdO_d[tt * 128:(tt + 1) * 128, h * 256:(h + 1) * 256])
                    P.dma(va_t, va_d[tt * 128:(tt + 1) * 128, h * 512:(h + 1) * 512])
                    rb = next_bank()
                    for kc in range(KC):
                        P.matmul(rb[:], RA[:, kc, tt * 128:(tt + 1) * 128], wr[:, kc, :], start=(kc == 0), stop=(kc == KC - 1))
                    P.activation(rsb, rb[:], AF.Silu)
                    for par in range(2):
                        kd_t = kE_t if par == 0 else kO_t
                        for j in range(2):
                            bank = ps[4 + j]
                            P.matmul(bank[:], kd_t[:, j * 128:(j + 1) * 128], va_t)
                            dcol = (tt * 8 + h * 2 + j) * 2 + par
                            P.stt(Sj[j], Sj[j], dec[:, dcol:dcol + 1], bank[:], ALU.mult, ALU.add)
                            P.copy(Sbf[par][j], Sj[j], eng="act")
                    ob = ps[3]
                    P.matmul(ob[:], qE[:, 0, tt * 128:(tt + 1) * 128], Sbf[0][0], start=True, stop=False)
                    P.matmul(ob[:], qE[:, 1, tt * 128:(tt + 1) * 128], Sbf[0][1], start=False, stop=False)
                    P.matmul(ob[:], qO[:, 0, tt * 128:(tt + 1) * 128], Sbf[1][0], start=False, stop=False)
                    P.matmul(ob[:], qO[:, 1, tt * 128:(tt + 1) * 128], Sbf[1][1], start=False, stop=True)
                    P.activation(t1f, ob[:], AF.Square)
                    P.scan(gbf, t1f, 0.0)
                    P.rsqrt(rs1, gbf[:, 511:512], EPS, 1.0 / 512.0)
                    P.stt(t1f, ob[:], rs1, gn_bc[:, h * 512:(h + 1) * 512], ALU.mult, ALU.mult)
                    P.tt(gbf, t1f, rsb, ALU.mult)
                    tb_ = ps[6]
                    for jj in range(4):
                        P.transpose(tb_[:, jj * 128:(jj + 1) * 128], gbf[:, jj * 128:(jj + 1) * 128], ident_f)
                    P.copy(RC[:, h * 4:(h + 1) * 4, tt * 128:(tt + 1) * 128],
                           tb_[:, 0:512].rearrange("p (a b) -> p a b", a=4), eng="act")
                if half == 0:
                    for j in range(2):
                        P.dma(st_d[(h * 2 + j) * 128:(h * 2 + j + 1) * 128, :], Sj[j])
            if l == 0:
                dump("glaoT%d" % half, RC, BF16)
            if stop_after == "d":
                return True

            tmpA = RDf[:, 0:4, :].rearrange("p (a b) c -> p a (b c)", b=2)
            tmpB = RDf[:, 4:8, :].rearrange("p (a b) c -> p a (b c)", b=2)
            for nb in range(8):
                def ev_ga(n, th, pa, msz):
                    P.activation(tmpA[:, n % 2, th * 512:(th + 1) * 512], pa, AF.Sigmoid)

                def ev_ya(n, th, pa, msz):
                    P.tt(tmpA[:, n % 2, th * 512:(th + 1) * 512], pa, tmpA[:, n % 2, th * 512:(th + 1) * 512], ALU.mult)

                def ev_gb(n, th, pa, msz):
                    P.activation(tmpB[:, n % 2, th * 512:(th + 1) * 512], pa, AF.Sigmoid)

                def ev_yb(n, th, pa, msz, nb=nb):
                    a = tmpA[:, n % 2, th * 512:(th + 1) * 512]
                    b_ = tmpB[:, n % 2, th * 512:(th + 1) * 512]
                    P.tt(b_, pa, b_, ALU.mult)
                    P.tt(a, a, b_, ALU.add)
                    mb = mixstg[:, (n * 2 + th) % 2, :]
                    P.copy(mb, a, eng="act")
                    P.dma(mixT_d[(nb * 2 + n % 2) * 128:(nb * 2 + n % 2 + 1) * 128, th * 512:(th + 1) * 512], mb)
                cs = slice(nb * 256, (nb + 1) * 256)
                linear_fm(w_in[l][:, C_GA + nb * 256:C_GA + (nb + 1) * 256], RA, KC, 256, ev_ga, blk=256)
                linear_fm(w_gla_o[l][:, cs], RC, KC, 256, ev_ya, blk=256)
                linear_fm(w_in[l][:, C_GB + nb * 256:C_GB + (nb + 1) * 256], RA, KC, 256, ev_gb, blk=256)
                linear_fm(w_sb_o[l][:, cs], RB, KC, 256, ev_yb, blk=256)
            if stop_after == "e":
                return True

            for kc in range(KC):
                P.dma(RA[:, kc, :], mixT_d[kc * 128:(kc + 1) * 128, :])
            if l == 0:
                dump("mixT%d" % half, RA[:], BF16)

            def ev_mo(n, th, pa, msz):
                P.copy(RBC[:, n, th * 512:(th + 1) * 512], pa, eng="dve")
                sumsq_accum(RBC[:, n, th * 512:(th + 1) * 512], n, th)
            linear_fm(w_out[l], RA, KC, D, ev_mo)
            resid_update(RBC, 32, xT_d)
            if stop_after == "f":
                return True

            norm_mod(48, 64, RA, xT_d)
            f1T = RD[:, 0:8, :]
            for fb in range(8):
                def ev_f1(n, th, pa, msz):
                    r_ = RDf[:, 8 + (n * 2 + th) % 2, :]
                    P.activation(r_, pa, AF.Relu)
                    P.tt(f1T[:, n % 8, th * 512:(th + 1) * 512], r_, r_, ALU.mult)
                linear_fm(w_ff1[l][:, fb * 1024:(fb + 1) * 1024], RA, KC, 1024, ev_f1)

                def ev_f2(n, th, pa, msz, fb=fb):
                    dst = RBC[:, n, th * 512:(th + 1) * 512]
                    if fb == 0:
                        P.copy(dst, pa, eng="act")
                    else:
                        P.tt(dst, dst, pa, ALU.add)
                    if fb == 7:
                        sumsq_accum(dst, n, th)
                linear_fm(w_ff2[l][fb * 1024:(fb + 1) * 1024, :], f1T, 8, D, ev_f2)
            resid_update(RBC, 80, xT_d)

            return False

        stop = False
        for l in range(depth):
            P.dma(gains[:], gainsT[l])
            P.dma(badT[:], b_adaT[l])
            P.dma(gn_bc[:], gn_bc_in[l])
            P.dma(wgu_sb[:], w_gu[l], q="pool")
            P.dma(bg_sb[:], b_gate[l], q="pool")
            pm = ps[3]
            for c0 in range(0, 6 * D, 512):
                wb = wload(w_ada[l][:, c0:c0 + 512], KC, 512)
                for j in range(4):
                    col = c0 // 128 + j
                    for kc in range(KC):
                        P.matmul(pm[:, col:col + 1], wb[:, kc, j * 128:(j + 1) * 128], cact[:, kc:kc + 1],
                                 start=(kc == 0), stop=(kc == KC - 1))
            P.tt(modT[:], pm[:, 0:96], badT[:], ALU.add)
            P.stt(coef[:, 0:16], modT[:, 16:32], 1.0, gains[:, 0:16], ALU.add, ALU.mult)
            P.copy(coef[:, 16:32], modT[:, 0:16])
            P.tt(coef[:, 32:48], modT[:, 32:48], gains[:, 16:32], ALU.mult)
            P.stt(coef[:, 48:64], modT[:, 64:80], 1.0, gains[:, 32:48], ALU.add, ALU.mult)
            P.copy(coef[:, 64:80], modT[:, 48:64])
            P.tt(coef[:, 80:96], modT[:, 80:96], gains[:, 48:64], ALU.mult)
            if l == 0:
                dump("modT", modT[:])

            for half in range(2):
                if layer_half(l, half):
                    stop = True
                    break
            if stop:
                break

        for tt16 in range(16):
            xT_d = xT_dd[tt16 // 8]
            tt = tt16 % 8
            for half in range(2):
                o = rdf((tt * 2 + half) % 2)
                for g in range(2):
                    kc0 = half * 8 + g * 4
                    src = rdf(2 + g)
                    P.dma(src[:, 0:512].rearrange("p (j t) -> p j t", j=4),
                          xT_d[kc0 * 128:(kc0 + 4) * 128, tt * 128:(tt + 1) * 128].rearrange("(j p) t -> p j t", p=128))
                    bank = next_bank()
                    for j in range(4):
                        P.transpose(bank[:, j * 128:(j + 1) * 128], src[:, j * 128:(j + 1) * 128], ident_f)
                    P.copy(o[:, g * 512:(g + 1) * 512], bank[:], eng=("act" if g else "dve"))
                P.dma(y_out[tt16 * 128:(tt16 + 1) * 128, half * 1024:(half + 1) * 1024], o, final=True)
        P.emit()
    return nc


def make_consts():
    c = np.zeros((128, 1024), np.float32)
    p = np.arange(128)[:, None]
    f = np.arange(128)[None, :]
    c[:, 0:128] = (p == f)
    c[:, 128:256] = (f < p)
    c[:, 256] = (np.arange(128) < 64)
    c[:, 257] = (np.arange(128) >= 64)
    c[:, 384:512] = 1.0 / 2048.0
    same = (p // 64) == (f // 64)
    c[:, 512:640] = np.where((p > f) & same, -1.0 / 16.0, 0.0)
    c[:, 640] = np.where(np.arange(128) < 64, -1.0 / 16.0, 0.0)
    c[:, 641] = np.where(np.arange(128) >= 64, -1.0 / 16.0, 0.0)
    c[:, 768:896] = 1.0
    return c


def make_in_maps(inputs, n_cores=8, depth=DEPTH):
    f32 = lambda a: np.ascontiguousarray(np.asarray(a, dtype=np.float32))
    x = f32(inputs["x"])
    c = f32(inputs["c"])
    L = DEPTH
    shared0 = {
        "w_ada": f32(inputs["w_ada"]),
        "w_in": f32(inputs["w_in"]),
        "w_gate_up": f32(inputs["w_gate_up"]),
        "w_gla_o": f32(inputs["w_gla_o"]),
        "w_sb_o": f32(inputs["w_sb_o"]),
        "w_out": f32(inputs["w_out"]),
        "w_ff1": f32(inputs["w_ff1"]),
        "w_ff2": f32(inputs["w_ff2"]),
        "b_adaT": np.ascontiguousarray(f32(inputs["b_ada"]).reshape(L, 96, 128).transpose(0, 2, 1)),
        "gainsT": np.ascontiguousarray(f32(inputs["norm_gains"]).reshape(L, 4, KC, 128).transpose(0, 3, 1, 2).reshape(L, 128, 4 * KC)),
        "b_gate": f32(inputs["b_gate"]).reshape(L, 1, 1024),
        "gn_bc": np.ascontiguousarray(np.broadcast_to(f32(inputs["gla_norm_gain"]).reshape(L, 1, 2048), (L, 128, 2048))),
    }
    shared = {k: np.ascontiguousarray(v[:depth]) for k, v in shared0.items()}
    shared["consts"] = make_consts()
    maps = []
    for core in range(n_cores):
        b = core % 4
        m = dict(shared)
        m["x"] = np.ascontiguousarray(x[b])
        m["cT"] = np.ascontiguousarray(c[b].reshape(KC, 128).T)
        maps.append(m)
    return maps


_NC_CACHE = {}
N_LAUNCH_CORES = 8


def kernel(**inputs):
    n = N_LAUNCH_CORES
    if "nc" not in _NC_CACHE:
        _NC_CACHE["nc"] = build_program(n)
    nc = _NC_CACHE["nc"]
    maps = make_in_maps(inputs, n)
    res = run_bass_kernel_spmd(nc, maps, core_ids=list(range(n)))
    out = np.empty((4, 2 * T, D), np.float32)
    for b in range(4):
        out[b] = np.asarray(res.results[b + 4]["y"], dtype=np.float32)
    if os.environ.get("KDIAG"):
        for b in range(4):
            a0 = np.asarray(res.results[b]["y"], dtype=np.float32)
            print("[kdiag] batch", b, "core", b, "vs core", b + 4, "maxabs diff", float(np.abs(a0 - out[b]).max()), flush=True)
    return out
```

```python
import os
import numpy as np
import ml_dtypes
from contextlib import ExitStack
import concourse.bass as bass
import concourse.mybir as mybir
from concourse.bass_utils import run_bass_kernel_spmd

F32 = mybir.dt.float32
BF16 = mybir.dt.bfloat16
AF = mybir.ActivationFunctionType
ALU = mybir.AluOpType

COMPUTE = ("pe", "act", "dve", "pool")
ALL_ENG = ("pe", "act", "dve", "pool", "sp")
N_DMA_SEMS = 24
N_CC_SEMS = 4
CC_INC = 1

D = 2048
T = 1024
KC = 16
DEPTH = 4
EPS = 1e-6
IN_COLS = 16400
C_QA, C_KA, C_VA, C_RA, C_AL, C_QB, C_KB, C_VB, C_GA, C_GB = (
    0, 1024, 2048, 4096, 6144, 6160, 8208, 10256, 12304, 14352)


class _Ins:
    __slots__ = ("eng", "fn", "waits", "signal", "ext", "idx", "clock")

    def __init__(self, eng, fn):
        self.eng = eng
        self.fn = fn
        self.waits = []
        self.signal = False
        self.ext = None
        self.idx = -1
        self.clock = None


def _ap_range(ap):
    t = ap.tensor
    shape = list(t.shape)
    rsz = 1
    for s in shape[1:]:
        rsz *= s
    esz = mybir.dt.size(ap.dtype)
    off = int(ap.offset)
    r_lo = r_hi = off // rsz
    c_lo = c_hi = off % rsz
    for step, cnt in ap.ap:
        step = int(step)
        cnt = int(cnt)
        if cnt <= 1 or step == 0:
            continue
        ext = step * (cnt - 1)
        if abs(step) >= rsz and step % rsz == 0:
            e = ext // rsz
            if e > 0:
                r_hi += e
            else:
                r_lo += e
        else:
            if ext > 0:
                c_hi += ext
            else:
                c_lo += ext
    return t.name, r_lo, r_hi + 1, c_lo * esz, (c_hi + 1) * esz


class Prog:
    def __init__(self, nc):
        self.nc = nc
        self.ins = {e: [] for e in ALL_ENG}
        self.regs = {}
        self.known = {e: {} for e in ALL_ENG}
        self.known_ext = {e: set() for e in ALL_ENG}
        self.exts = []
        self.slot_last = {"d": [None] * N_DMA_SEMS, "c": [None] * N_CC_SEMS}
        self.slot_cnt = {"d": [0] * N_DMA_SEMS, "c": [0] * N_CC_SEMS}
        self.next_slot = {"d": 0, "c": 0}
        self.out_exts = []

    def _add_dep(self, ins, dep, raw):
        if dep is None:
            return
        eng = ins.eng
        if dep[0] == "x":
            if dep[1] in self.known_ext[eng]:
                return
            ins.waits.append(dep)
            return
        _, e2, idx = dep
        if e2 == eng:
            if not raw or eng == "pe":
                return
            if ins.idx - idx > 2:
                return
            ins.waits.append(dep)
            return
        if self.known[eng].get(e2, -1) >= idx:
            return
        ins.waits.append(dep)

    def add(self, eng, fn, reads=(), writes=(), ext=None, out=False):
        ins = _Ins(eng, fn)
        ins.idx = len(self.ins[eng])
        me = ("e", eng, ins.idx)
        prev_wait = None
        if ext is not None:
            kind = ext
            n = N_DMA_SEMS if kind == "d" else N_CC_SEMS
            xid = len(self.exts)
            slot = self.next_slot[kind]
            self.next_slot[kind] = (slot + 1) % n
            prev = self.slot_last[kind][slot]
            if prev is not None and prev not in self.known_ext[eng]:
                prev_wait = ("x", prev)
            self.slot_cnt[kind][slot] += 16 if kind == "d" else CC_INC
            self.exts.append((kind, slot, self.slot_cnt[kind][slot]))
            self.slot_last[kind][slot] = xid
            ins.ext = xid
            me = ("x", xid)
            if out:
                self.out_exts.append(xid)
        rr = [_ap_range(a) for a in reads]
        wr = [_ap_range(a) for a in writes]
        for (nm, r0, r1, c0, c1) in rr:
            for ent in self.regs.get(nm, ()):
                if ent[0] < r1 and r0 < ent[1] and ent[2] < c1 and c0 < ent[3]:
                    self._add_dep(ins, ent[4], True)
        for (nm, r0, r1, c0, c1) in wr:
            for ent in self.regs.get(nm, ()):
                if ent[0] < r1 and r0 < ent[1] and ent[2] < c1 and c0 < ent[3]:
                    self._add_dep(ins, ent[4], False)
                    for rk, rv in ent[5].items():
                        if isinstance(rk, tuple):
                            self._add_dep(ins, rk, False)
                        else:
                            self._add_dep(ins, ("e", rk, rv), False)
        best = {}
        xs = []
        for w in ins.waits:
            if w[0] == "e":
                if best.get(w[1], -1) < w[2]:
                    best[w[1]] = w[2]
            elif w not in xs:
                xs.append(w)
        if prev_wait is not None and prev_wait not in xs:
            xs.append(prev_wait)
        ins.waits = [("e", e, i) for e, i in best.items()] + xs
        kn = self.known[eng]
        kx = self.known_ext[eng]
        for w in ins.waits:
            if w[0] == "e":
                _, e2, i2 = w
                src = self.ins[e2][i2]
                src.signal = True
                if e2 != eng:
                    if kn.get(e2, -1) < i2:
                        kn[e2] = i2
                    ck = src.clock
                    for e3, i3 in ck[0].items():
                        if e3 != eng and kn.get(e3, -1) < i3:
                            kn[e3] = i3
                    kx |= ck[1]
            else:
                kx.add(w[1])
        ins.clock = (dict(kn), set(kx))
        for (nm, r0, r1, c0, c1) in rr:
            lst = self.regs.setdefault(nm, [])
            hit = None
            for ent in lst:
                if ent[0] <= r0 and r1 <= ent[1] and ent[2] <= c0 and c1 <= ent[3]:
                    hit = ent
                    break
            if hit is None:
                hit = [r0, r1, c0, c1, None, {}]
                lst.append(hit)
            if me[0] == "x":
                hit[5][me] = True
            else:
                hit[5][eng] = ins.idx
        for (nm, r0, r1, c0, c1) in wr:
            lst = self.regs.setdefault(nm, [])
            keep = [ent for ent in lst
                    if not (r0 <= ent[0] and ent[1] <= r1 and c0 <= ent[2] and ent[3] <= c1)]
            keep.append([r0, r1, c0, c1, me, {}])
            self.regs[nm] = keep
        self.ins[eng].append(ins)
        return ins

    def matmul(self, out, lhsT, rhs, start=True, stop=True):
        return self.add("pe", lambda e: e.matmul(out, lhsT, rhs, start=start, stop=stop),
                        reads=[lhsT, rhs], writes=[out])

    def transpose(self, out, in_, ident):
        return self.add("pe", lambda e: e.transpose(out, in_, ident), reads=[in_, ident], writes=[out])

    def activation(self, out, in_, func, bias=None, scale=None, accum_out=None, eng="act"):
        kw = {}
        rd = [in_]
        wr = [out]
        if bias is not None:
            kw["bias"] = bias
            if not isinstance(bias, (int, float)):
                rd.append(bias)
        if scale is not None:
            kw["scale"] = scale
            if not isinstance(scale, (int, float)):
                rd.append(scale)
        if accum_out is not None:
            kw["accum_out"] = accum_out
            wr.append(accum_out)
        return self.add(eng, lambda e: e.activation(out=out, in_=in_, func=func, **kw), reads=rd, writes=wr)

    def tt(self, out, in0, in1, op, eng="dve"):
        return self.add(eng, lambda e: e.tensor_tensor(out=out, in0=in0, in1=in1, op=op),
                        reads=[in0, in1], writes=[out])

    def ts(self, out, in0, s1, s2, op0, op1=None, eng="dve"):
        rd = [in0] + [s for s in (s1, s2) if s is not None and not isinstance(s, (int, float))]
        if op1 is None:
            return self.add(eng, lambda e: e.tensor_scalar(out=out, in0=in0, scalar1=s1, scalar2=None, op0=op0),
                            reads=rd, writes=[out])
        return self.add(eng, lambda e: e.tensor_scalar(out=out, in0=in0, scalar1=s1, scalar2=s2, op0=op0, op1=op1),
                        reads=rd, writes=[out])

    def stt(self, out, in0, scalar, in1, op0, op1, eng="dve"):
        rd = [in0, in1] + ([] if isinstance(scalar, (int, float)) else [scalar])
        return self.add(eng, lambda e: e.scalar_tensor_tensor(out=out, in0=in0, scalar=scalar, in1=in1, op0=op0, op1=op1),
                        reads=rd, writes=[out])

    def copy(self, out, in_, eng="dve"):
        if eng == "act":
            return self.activation(out, in_, AF.Copy)
        return self.add(eng, lambda e: e.tensor_copy(out=out, in_=in_), reads=[in_], writes=[out])

    def scan(self, out, data, initial, eng="dve"):
        rd = [data] + ([] if isinstance(initial, (int, float)) else [initial])
        return self.add(eng, lambda e: e.tensor_tensor_scan(out=out, data0=data, data1=data, initial=initial,
                                                          op0=ALU.add, op1=ALU.bypass), reads=rd, writes=[out])

    def rsqrt(self, out, in_, eps, scale_in=1.0):
        self.activation(out, in_, AF.Ln, bias=eps, scale=scale_in)
        return self.activation(out, out, AF.Exp, scale=-0.5)

    def memset(self, out, val, eng="dve"):
        return self.add(eng, lambda e: e.memset(out, val), writes=[out])

    def dma(self, out, in_, q="sp", final=False):
        return self.add(q, lambda e: e.dma_start(out=out, in_=in_), reads=[in_], writes=[out], ext="d", out=final)

    def allgather(self, out, in_, groups):
        return self.add("pool", lambda e: e.collective_compute("AllGather", ALU.bypass, replica_groups=groups,
                                                               ins=[in_], outs=[out]),
                        reads=[in_], writes=[out], ext="c")

    def emit(self):
        nc = self.nc
        with ExitStack() as st:
            esem = {e: st.enter_context(nc.semaphore("s_" + e)) for e in COMPUTE}
            xsem = {"d": [st.enter_context(nc.semaphore("d_%d" % i)) for i in range(N_DMA_SEMS)],
                    "c": [st.enter_context(nc.semaphore("c_%d" % i)) for i in range(N_CC_SEMS)]}
            cnt = {}
            for e in COMPUTE:
                c = 0
                arr = []
                for ins in self.ins[e]:
                    if ins.signal:
                        c += 1
                    arr.append(c)
                cnt[e] = arr

            def run(ename, h):
                for ins in self.ins[ename]:
                    for w in ins.waits:
                        if w[0] == "e":
                            h.wait_ge(esem[w[1]], cnt[w[1]][w[2]])
                        else:
                            kind, slot, val = self.exts[w[1]]
                            h.wait_ge(xsem[kind][slot], val)
                    bi = ins.fn(h)
                    if ins.ext is not None:
                        kind, slot, val = self.exts[ins.ext]
                        bi.then_inc(xsem[kind][slot], 16 if kind == "d" else CC_INC)
                    elif ins.signal:
                        bi.then_inc(esem[ename], 1)
                if ename == "sp":
                    for xid in self.out_exts:
                        kind, slot, val = self.exts[xid]
                        h.wait_ge(xsem[kind][slot], val)

            with nc.Block() as block:
                @block.tensor
                def _(h):
                    run("pe", h)

                @block.scalar
                def _(h):
                    run("act", h)

                @block.vector
                def _(h):
                    run("dve", h)

                @block.gpsimd
                def _(h):
                    run("pool", h)

                @block.sync
                def _(h):
                    run("sp", h)


_DSZ = {}


def build_program(n_cores=8, depth=DEPTH, dbg=None, stop_after=None):
    dbg = dbg or set()
    nc = bass.Bass("TRN2", target_bir_lowering=False)
    groups = [[2 * i, 2 * i + 1] for i in range(n_cores // 2)]
    P = Prog(nc)

    def din(name, shape, dt=F32):
        return nc.dram_tensor(name, list(shape), dt, kind="ExternalInput").ap()

    def dout(name, shape, dt=F32):
        return nc.dram_tensor(name, list(shape), dt, kind="ExternalOutput").ap()

    def dscr(name, shape, dt):
        return nc.dram_tensor(name, list(shape), dt).ap()

    x_in = din("x", [2 * T, D])
    cT_in = din("cT", [128, KC])
    w_ada = din("w_ada", [depth, D, 6 * D])
    b_adaT = din("b_adaT", [depth, 128, 96])
    gainsT = din("gainsT", [depth, 128, 4 * KC])
    w_in = din("w_in", [depth, D, IN_COLS])
    w_gu = din("w_gate_up", [depth, 16, 1024])
    b_gate = din("b_gate", [depth, 1, 1024])
    gn_bc_in = din("gn_bc", [depth, 128, 2048])
    w_gla_o = din("w_gla_o", [depth, D, D])
    w_sb_o = din("w_sb_o", [depth, D, D])
    w_out = din("w_out", [depth, D, D])
    w_ff1 = din("w_ff1", [depth, D, 4 * D])
    w_ff2 = din("w_ff2", [depth, 4 * D, D])
    consts_in = din("consts", [128, 1024])
    y_out = dout("y", [2 * T, D])

    xT_dd = [dscr("xT_d%d" % i, [D, T], F32) for i in range(2)]
    kT_dd = [dscr("kT_d%d" % i, [D, T], BF16) for i in range(2)]
    v_dd = [dscr("v_d%d" % i, [T, D], BF16) for i in range(2)]
    st_d = dscr("st_d", [1024, 512], F32)
    kdE_d = dscr("kdE_d", [T, 1024], BF16)
    kdO_d = dscr("kdO_d", [T, 1024], BF16)
    va_d = dscr("va_d", [T, D], BF16)
    mixT_d = dscr("mixT_d", [D, T], BF16)

    SCALE_SB = 128.0 ** -0.5

    with ExitStack() as st:
        def sb(name, shape, dt):
            return st.enter_context(nc.sbuf_tensor(name, list(shape), dt))

        ps = [st.enter_context(nc.psum_tensor("ps%d" % i, [128, 512], F32)) for i in range(8)]

        cst = sb("cst", [128, 1024], F32)
        ident_f = cst[:, 0:128]
        maskf = cst[:, 128:256]
        rowE = cst[:, 256:257]
        rowO = cst[:, 257:258]
        cstb = sb("cstb", [128, 768], BF16)
        ident_b = cstb[:, 0:128]
        ones_b = cstb[:, 128:256]
        m1_b = cstb[:, 256:384]
        csel_b = cstb[:, 384:386]
        one_row = cstb[0:1, 512:640]
        cact = sb("cact", [128, KC], BF16)
        modT = sb("modT", [128, 96], F32)
        badT = sb("badT", [128, 96], F32)
        gains = sb("gains", [128, 4 * KC], F32)
        coef = sb("coef", [128, 6 * KC], F32)
        smallf = sb("smallf", [128, 64], F32)
        dec = sb("dec", [128, 128], F32)
        carry = sb("carry", [128, 8], F32)
        alT = sb("alT", [16, 1024], BF16)
        wgu_sb = sb("wgu_sb", [16, 1024], BF16)
        bg_sb = sb("bg_sb", [1, 1024], BF16)
        gn_bc = sb("gn_bc_sb", [128, 2048], F32)
        rstd_bc = sb("rstd_bc", [128, 2, 512], F32)
        mixstg = sb("mixstg", [128, 2, 512], BF16)
        wbuf = [sb("wbuf%d" % i, [128, KC, 512], BF16) for i in range(3)]
        RA = sb("RA", [128, KC, T], BF16)
        RBC = sb("RBC", [128, KC, T], F32)
        RD = sb("RD", [128, KC, T], BF16)
        RB = RBC[:, 0:8, :].bitcast(BF16).rearrange("p a (b c) -> p (a b) c", c=T)
        RC = RBC[:, 8:16, :].bitcast(BF16).rearrange("p a (b c) -> p (a b) c", c=T)
        RDf = RD[:].bitcast(F32)

        def rdf(slot):
            return RDf[:, 2 * slot:2 * slot + 2, :].rearrange("p a b -> p (a b)")

        def rdb(slot):
            return RD[:, slot, :]

        st_w = {"i": 0}

        def wload(src2d, kc_n, ncols):
            wb = wbuf[st_w["i"] % 3]
            st_w["i"] += 1
            srcv = src2d.rearrange("(kc p) n -> p kc n", p=128)
            step = 4 if ncols >= 256 else kc_n
            for k0 in range(0, kc_n, step):
                k1 = min(kc_n, k0 + step)
                P.dma(wb[:, k0:k1, 0:ncols], srcv[:, k0:k1, :], q="pool")
            return wb

        lin_bank = {"i": 0}

        def next_bank():
            b = ps[lin_bank["i"] % 3]
            lin_bank["i"] += 1
            return b

        def linear_fm(wsrc, in_sb, kc_n, ncols, evac, blk=512):
            for c0 in range(0, ncols, blk):
                cw = min(blk, ncols - c0)
                wb = wload(wsrc[:, c0:c0 + cw], kc_n, cw)
                for j0 in range(0, cw, 128):
                    msz = min(128, cw - j0)
                    for th in range(2):
                        bank = next_bank()
                        for kc in range(kc_n):
                            P.matmul(bank[0:msz, :], wb[:, kc, j0:j0 + msz], in_sb[:, kc, th * 512:(th + 1) * 512],
                                     start=(kc == 0), stop=(kc == kc_n - 1))
                        evac((c0 + j0) // 128, th, bank[0:msz, :], msz)

        def linear_tm(wsrc, in_sb, kc_n, ncols, evac):
            for c0 in range(0, ncols, 512):
                cw = min(512, ncols - c0)
                wb = wload(wsrc[:, c0:c0 + cw], kc_n, cw)
                for tt in range(8):
                    bank = next_bank()
                    for kc in range(kc_n):
                        P.matmul(bank[:, 0:cw], in_sb[:, kc, tt * 128:(tt + 1) * 128], wb[:, kc, 0:cw],
                                 start=(kc == 0), stop=(kc == kc_n - 1))
                    evac(tt, c0, bank[:, 0:cw], cw)

        def dump(name, ap_src, dt=F32):
            if name not in dbg:
                return
            o = dout("dbg_" + name, list(ap_src.shape), dt)
            P.dma(o, ap_src, final=True)

        P.dma(cst[:], consts_in)
        P.copy(cstb[:, 0:128], cst[:, 0:128])
        P.copy(cstb[:, 128:768], cst[:, 384:1024])
        P.dma(smallf[:, 0:KC], cT_in)
        P.activation(cact[:], smallf[:, 0:KC], AF.Silu)
        for tt16 in range(16):
            xT_d = xT_dd[tt16 // 8]
            tt = tt16 % 8
            xt = rdf(tt % 2)
            xt2 = rdf(2 + tt % 2)
            P.dma(xt, x_in[tt16 * 128:(tt16 + 1) * 128, 0:1024])
            P.dma(xt2, x_in[tt16 * 128:(tt16 + 1) * 128, 1024:2048])
            for half, src in enumerate((xt, xt2)):
                for g in range(2):
                    bank = next_bank()
                    for j in range(4):
                        P.transpose(bank[:, j * 128:(j + 1) * 128], src[:, (g * 4 + j) * 128:(g * 4 + j + 1) * 128], ident_f)
                    o = rdf(4 + (half * 2 + g) % 2)
                    P.copy(o[:, 0:512], bank[:], eng=("act" if g else "dve"))
                    kc0 = half * 8 + g * 4
                    P.dma(xT_d[kc0 * 128:(kc0 + 4) * 128, tt * 128:(tt + 1) * 128].rearrange("(j p) t -> p j t", p=128),
                          o[:, 0:512].rearrange("p (j t) -> p j t", j=4))

        def sumsq_bank(th):
            return ps[4 + th]

        def norm_mod(a_col, sh_col, dst, xT_d):
            for kc in range(KC):
                xs = rdf(kc % 3)
                P.dma(xs, xT_d[kc * 128:(kc + 1) * 128, :])
                for th in range(2):
                    sq = rdb(12 + (kc * 2 + th) % 4)
                    P.activation(sq[:, 0:512], xs[:, th * 512:(th + 1) * 512], AF.Square)
                    P.matmul(sumsq_bank(th)[:], ones_b, sq[:, 0:512], start=(kc == 0), stop=(kc == KC - 1))
            for th in range(2):
                P.rsqrt(rstd_bc[:, th, :], sumsq_bank(th)[:], EPS)
            for kc in range(KC):
                xs = rdf(kc % 3)
                P.dma(xs, xT_d[kc * 128:(kc + 1) * 128, :])
                t1 = rdf(3 + kc % 2)
                for th in range(2):
                    P.stt(t1[:, th * 512:(th + 1) * 512], xs[:, th * 512:(th + 1) * 512],
                          coef[:, a_col + kc:a_col + kc + 1], rstd_bc[:, th, :], ALU.mult, ALU.mult)
                P.activation(dst[:, kc, :], t1, AF.Identity, bias=coef[:, sh_col + kc:sh_col + kc + 1])

        def sumsq_accum(src_f32, n, th):
            sq = rdb(12 + (n * 2 + th) % 4)
            P.activation(sq[:, 0:512], src_f32, AF.Square)
            P.matmul(sumsq_bank(th)[:], ones_b, sq[:, 0:512], start=(n == 0), stop=(n == KC - 1))

        def resid_update(yT, g_col, xT_d):
            for th in range(2):
                P.rsqrt(rstd_bc[:, th, :], sumsq_bank(th)[:], EPS)
            for kc in range(KC):
                xs = rdf(kc % 3)
                P.dma(xs, xT_d[kc * 128:(kc + 1) * 128, :])
                t1 = rdf(3 + kc % 2)
                for th in range(2):
                    P.stt(t1[:, th * 512:(th + 1) * 512], yT[:, kc, th * 512:(th + 1) * 512],
                          coef[:, g_col + kc:g_col + kc + 1], rstd_bc[:, th, :], ALU.mult, ALU.mult)
                P.tt(xs, xs, t1, ALU.add)
                P.dma(xT_d[kc * 128:(kc + 1) * 128, :], xs)

        def layer_half(l, half):
            xT_d = xT_dd[half]
            kT_in = kT_dd[half]
            v_in = v_dd[half]
            norm_mod(0, 16, RA, xT_d)
            if l == 0:
                dump("hT%d" % half, RA[:], BF16)
            if stop_after == "h":
                return True

            def ev_k(n, th, pa, msz):
                s = rdb(10 + n % 2)
                P.copy(s[:, th * 512:(th + 1) * 512], pa, eng=("act" if th else "dve"))
                if th == 1:
                    P.dma(kT_in[n * 128:(n + 1) * 128, :], s)
            linear_fm(w_in[l][:, C_KB:C_KB + D], RA, KC, D, ev_k)

            vi = {"i": 0}

            def ev_v(tt, c0, pa, cw):
                s = rdb(10 + vi["i"] % 4)
                vi["i"] += 1
                P.copy(s[:, 0:cw], pa, eng=("act" if tt % 2 else "dve"))
                P.dma(v_in[tt * 128:(tt + 1) * 128, c0:c0 + cw], s[:, 0:cw])
            linear_tm(w_in[l][:, C_VB:C_VB + D], RA, KC, D, ev_v)
            if stop_after == "a":
                return True

            kdE = RB[:, 0:8, :]
            kdO = RB[:, 8:16, :]
            va = RC.rearrange("p (a b) c -> p a (b c)", b=2)
            egb = RD[:, 0:8, :]

            def ev_al(n, th, pa, msz):
                P.copy(alT[0:16, th * 512:(th + 1) * 512], pa)
            linear_fm(w_in[l][:, C_AL:C_AL + 16], RA, KC, 16, ev_al)
            for tt in range(8):
                en = rdf(4 + tt % 2)
                spb = rdb(12 + tt % 2)
                for blk in range(2):
                    pb = ps[4 + blk]
                    P.matmul(pb[:], alT[0:16, tt * 128:(tt + 1) * 128], wgu_sb[0:16, blk * 512:(blk + 1) * 512],
                             start=True, stop=False)
                    P.matmul(pb[:], one_row, bg_sb[0:1, blk * 512:(blk + 1) * 512], start=False, stop=True)
                    P.activation(en[:, blk * 512:(blk + 1) * 512], pb[:], AF.Exp, scale=-1.0)
                    P.activation(spb[:, blk * 512:(blk + 1) * 512], en[:, blk * 512:(blk + 1) * 512], AF.Ln, bias=1.0)
                for blk in range(2):
                    gb = ps[4 + blk]
                    P.matmul(gb[:], m1_b, spb[:, blk * 512:(blk + 1) * 512])
                    P.activation(egb[:, tt, blk * 512:(blk + 1) * 512], gb[:], AF.Exp)
                for fc in range(8):
                    col = (tt * 8 + fc) * 2
                    P.matmul(ps[3][:, col:col + 2], spb[:, fc * 128:(fc + 1) * 128], csel_b)
            P.activation(dec[:], ps[3][:, 0:128], AF.Exp)

            def ev_ka(tt, c0, pa, cw):
                P.stt(kdE[:, tt, c0:c0 + cw], pa, rowE, egb[:, tt, c0:c0 + cw], ALU.mult, ALU.mult)
                P.stt(kdO[:, tt, c0:c0 + cw], pa, rowO, egb[:, tt, c0:c0 + cw], ALU.mult, ALU.mult)
                P.dma(kdE_d[tt * 128:(tt + 1) * 128, c0:c0 + cw], kdE[:, tt, c0:c0 + cw])
                P.dma(kdO_d[tt * 128:(tt + 1) * 128, c0:c0 + cw], kdO[:, tt, c0:c0 + cw])
            linear_tm(w_in[l][:, C_KA:C_KA + 1024], RA, KC, 1024, ev_ka)

            def ev_va(tt, c0, pa, cw):
                P.copy(va[:, tt, c0:c0 + cw], pa, eng=("act" if tt % 2 else "dve"))
                P.dma(va_d[tt * 128:(tt + 1) * 128, c0:c0 + cw], va[:, tt, c0:c0 + cw])
            linear_tm(w_in[l][:, C_VA:C_VA + D], RA, KC, D, ev_va)
            if stop_after == "b":
                return True

            qT4 = RD[:, 0:4, :]
            kvbuf = [[RD[:, 4 + 4 * s + i, :] for i in range(4)] for s in range(2)]
            RCf = RBC[:, 8:16, :].rearrange("p a (b c) -> p (a b) c", c=512)
            tset = [[RCf[:, 4 * s + i, :] for i in range(4)] for s in range(3)]
            wbf = [RCf[:, 12, :], RCf[:, 13, :], RDf[:, 14, :]]
            wT_sb = [RC[:, 14, :].rearrange("p (b q) -> p b q", b=2), RC[:, 15, :].rearrange("p (b q) -> p b q", b=2),
                     RD[:, 12, :].rearrange("p (b q) -> p b q", b=2), RD[:, 13, :].rearrange("p (b q) -> p b q", b=2)]
            tile_i = 0
            grp_i = 0
            for hg in range(4):
                def ev_q(n, th, pa, msz):
                    P.copy(qT4[:, n % 4, th * 512:(th + 1) * 512], pa, eng=("act" if th else "dve"))
                linear_fm(w_in[l][:, C_QB + hg * 512:C_QB + (hg + 1) * 512], RA, KC, 512, ev_q)
                tiles = []
                for hh in range(4):
                    h = hg * 4 + hh
                    for qg in range(2):
                        chunks = [("o", qg, True)] + [("o", ci, False) for ci in reversed(range(qg))]
                        if half == 1:
                            chunks += [("p", 1, False), ("p", 0, False)]
                        for cidx, (srcn, ci, diag) in enumerate(chunks):
                            g = dict(h=h, hh=hh, qg=qg, cidx=cidx, nch=len(chunks), srcn=srcn, ci=ci, diag=diag, gi=grp_i)
                            grp_i += 1
                            for i in range(4):
                                tiles.append(dict(g=g, i=i, t=tile_i, first_of_head=(qg == 0 and cidx == 0 and i == 0),
                                                  last_of_group=(i == 3)))
                                tile_i += 1

                def kv_of(h):
                    kTo, kTp, vo_, vp_ = kvbuf[h % 2]
                    return (kTo, kTp, vo_.rearrange("p (kb d) -> p kb d", d=128), vp_.rearrange("p (kb d) -> p kb d", d=128))

                NSET = 8
                _sets = [(RCf[:, 2 * s, :], RCf[:, 2 * s + 1, :]) for s in range(7)] + [(RDf[:, 14, :], RDf[:, 15, :])]

                def bufs(t):
                    return _sets[t % NSET]

                def info(tl):
                    g, i = tl["g"], tl["i"]
                    return g, i, tl["t"], ((i + 1) * 128 if g["diag"] else 512), g["qg"] * 4 + i

                def sA1(tl):
                    g, i, t, wd, qt = info(tl)
                    h, hh, ci = g["h"], g["hh"], g["ci"]
                    kTo, kTp, vo, vp = kv_of(h)
                    if tl["first_of_head"]:
                        P.dma(kTo, kT_in[h * 128:(h + 1) * 128, :])
                        P.dma(vo, v_in[:, h * 128:(h + 1) * 128].rearrange("(kb p) d -> p kb d", p=128))
                        if half == 1:
                            P.dma(kTp, kT_dd[0][h * 128:(h + 1) * 128, :])
                            P.dma(vp, v_dd[0][:, h * 128:(h + 1) * 128].rearrange("(kb p) d -> p kb d", p=128))
                    kT = kTo if g["srcn"] == "o" else kTp
                    E, L = bufs(t)
                    zb = ps[t % 3]
                    P.matmul(zb[:, 0:wd], qT4[:, hh, qt * 128:(qt + 1) * 128], kT[:, ci * 512:ci * 512 + wd])
                    P.activation(E[:, 0:wd], zb[:, 0:wd], AF.Exp, scale=SCALE_SB)

                def sA2(tl):
                    g, i, t, wd, qt = info(tl)
                    E, L = bufs(t)
                    P.activation(L[:, 0:wd], E[:, 0:wd], AF.Ln, bias=1.0)

                def sA3(tl):
                    g, i, t, wd, qt = info(tl)
                    E, L = bufs(t)
                    if g["diag"]:
                        P.tt(L[:, i * 128:(i + 1) * 128], L[:, i * 128:(i + 1) * 128], maskf, ALU.mult)
                        P.tt(E[:, i * 128:(i + 1) * 128], E[:, i * 128:(i + 1) * 128], maskf, ALU.mult)
                    P.scan(L[:, 0:wd][:, ::-1], L[:, 0:wd][:, ::-1], 0.0 if g["diag"] else carry[:, qt:qt + 1])

                def sCarry(tl):
                    g, i, t, wd, qt = info(tl)
                    E, L = bufs(t)
                    P.copy(carry[:, qt:qt + 1], L[:, 0:1])

                def sB1(tl):
                    g, i, t, wd, qt = info(tl)
                    E, L = bufs(t)
                    P.activation(L[:, 0:wd], L[:, 0:wd], AF.Exp, scale=-1.0)

                def sB2(tl):
                    g, i, t, wd, qt = info(tl)
                    E, L = bufs(t)
                    P.tt(E[:, 0:wd], E[:, 0:wd], L[:, 0:wd], ALU.mult)
                    for blk in range(wd // 128):
                        P.transpose(ps[4 + blk][:, i * 128:(i + 1) * 128], E[:, blk * 128:(blk + 1) * 128], ident_f)

                def stageC(g):
                    h, qg, ci, diag = g["h"], g["qg"], g["ci"], g["diag"]
                    kTo, kTp, vo, vp = kv_of(h)
                    vv = vo if g["srcn"] == "o" else vp
                    wsb = [wT_sb[2 * (g["gi"] % 2)], wT_sb[2 * (g["gi"] % 2) + 1]]
                    oT = ps[3]
                    for blk in range(4):
                        lo = blk * 128 if diag else 0
                        P.copy(wsb[blk // 2][:, blk % 2, lo:512], ps[4 + blk][:, lo:512], eng=("act" if blk % 2 else "dve"))
                        P.matmul(oT[:, lo:512], vv[:, ci * 4 + blk, :], wsb[blk // 2][:, blk % 2, lo:512],
                                 start=(g["cidx"] == 0 and blk == 0), stop=(g["cidx"] == g["nch"] - 1 and blk == 3))
                    if g["cidx"] == g["nch"] - 1:
                        P.copy(RB[:, h, qg * 512:(qg + 1) * 512], oT[:], eng="act")

                nt = len(tiles)
                pendC = None
                for k in range(nt + 7):
                    if k < nt:
                        sA1(tiles[k])
                    if 0 <= k - 1 < nt:
                        sA2(tiles[k - 1])
                    if 0 <= k - 3 < nt:
                        sA3(tiles[k - 3])
                    if pendC is not None:
                        stageC(pendC)
                        pendC = None
                    if 0 <= k - 5 < nt:
                        sCarry(tiles[k - 5])
                        sB1(tiles[k - 5])
                    if 0 <= k - 7 < nt:
                        tl = tiles[k - 7]
                        sB2(tl)
                        if tl["last_of_group"]:
                            pendC = tl["g"]
                if pendC is not None:
                    stageC(pendC)
            if l == 0:
                dump("sboT%d" % half, RB, BF16)
            if stop_after == "c":
                return True

            qE = RD[:, 0:2, :]
            qO = RD[:, 2:4, :]
            Sj = [RDf[:, 4, :], RDf[:, 5, :]]
            Sbf = [[RD[:, 12, 0:512], RD[:, 12, 512:1024]], [RD[:, 13, 0:512], RD[:, 13, 512:1024]]]
            strm = [(RD[:, 14, 0:256], RD[:, 14, 256:512], RD[:, 14, 512:1024]),
                    (RD[:, 15, 0:256], RD[:, 15, 256:512], RD[:, 15, 512:1024])]
            t1f = RDf[:, 8, :]
            rsbs = [RD[:, 6, 0:512], RD[:, 6, 512:1024]]
            gbf = RDf[:, 9, :]
            pend_ep = [None]
            ssum = smallf[:, 32:33]
            rs1 = smallf[:, 33:34]
            for qs in range(4):
                P.memset(RD[:, qs, :], 0.0, eng="pool")
            for h in range(4):
                for j in range(2):
                    if half == 0:
                        P.memset(Sj[j], 0.0, eng="pool")
                    else:
                        P.dma(Sj[j], st_d[(h * 2 + j) * 128:(h * 2 + j + 1) * 128, :])

                def ev_qa(n, th, pa, msz):
                    pv = pa.rearrange("p (a b c) -> p a b c", b=2, c=64)
                    qe = qE[:, n, th * 512:(th + 1) * 512].rearrange("p (a b c) -> p a b c", b=2, c=64)
                    qo = qO[:, n, th * 512:(th + 1) * 512].rearrange("p (a b c) -> p a b c", b=2, c=64)
                    P.activation(qe[:, :, 0, :], pv[:, :, 0, :], AF.Identity, scale=1.0 / 16.0)
                    P.activation(qo[:, :, 1, :], pv[:, :, 1, :], AF.Identity, scale=1.0 / 16.0)
                linear_fm(w_in[l][:, C_QA + h * 256:C_QA + (h + 1) * 256], RA, KC, 256, ev_qa, blk=256)
                wr = wload(w_in[l][:, C_RA + h * 512:C_RA + (h + 1) * 512], KC, 512)
                for tt in range(8):
                    kE_t, kO_t, va_t = strm[tt % 2]
                    P.dma(kE_t, kdE_d[tt * 128:(tt + 1) * 128, h * 256:(h + 1) * 256])
                    P.dma(kO_t, kdO_d[tt * 128:(tt + 1) * 128, h * 256:(h + 1) * 256])
                    P.dma(va_t, va_d[tt * 128:(tt + 1) * 128, h * 512:(h + 1) * 512])
                    rb = next_bank()
                    for kc in range(KC):
                        P.matmul(rb[:], RA[:, kc, tt * 128:(tt + 1) * 128], wr[:, kc, :], start=(kc == 0), stop=(kc == KC - 1))
                    rsb = rsbs[tt % 2]
                    P.activation(rsb, rb[:], AF.Silu)
                    for par in range(2):
                        kd_t = kE_t if par == 0 else kO_t
                        for j in range(2):
                            bank = ps[4 + j]
                            P.matmul(bank[:], kd_t[:, j * 128:(j + 1) * 128], va_t)
                            dcol = (tt * 8 + h * 2 + j) * 2 + par
                            P.stt(Sj[j], Sj[j], dec[:, dcol:dcol + 1], bank[:], ALU.mult, ALU.add)
                            P.copy(Sbf[par][j], Sj[j], eng="act")
                    ob = ps[3] if tt % 2 == 0 else ps[7]
                    P.matmul(ob[:], qE[:, 0, tt * 128:(tt + 1) * 128], Sbf[0][0], start=True, stop=False)
                    P.matmul(ob[:], qE[:, 1, tt * 128:(tt + 1) * 128], Sbf[0][1], start=False, stop=False)
                    P.matmul(ob[:], qO[:, 0, tt * 128:(tt + 1) * 128], Sbf[1][0], start=False, stop=False)
                    P.matmul(ob[:], qO[:, 1, tt * 128:(tt + 1) * 128], Sbf[1][1], start=False, stop=True)

                    def epilogue(h=h, tt=tt, ob=ob, rsb=rsb):
                        P.activation(t1f, ob[:], AF.Square)
                        P.scan(gbf, t1f, 0.0)
                        P.rsqrt(rs1, gbf[:, 511:512], EPS, 1.0 / 512.0)
                        P.stt(t1f, ob[:], rs1, gn_bc[:, h * 512:(h + 1) * 512], ALU.mult, ALU.mult)
                        P.tt(gbf, t1f, rsb, ALU.mult)
                        tb_ = ps[6]
                        for jj in range(4):
                            P.transpose(tb_[:, jj * 128:(jj + 1) * 128], gbf[:, jj * 128:(jj + 1) * 128], ident_f)
                        P.copy(RC[:, h * 4:(h + 1) * 4, tt * 128:(tt + 1) * 128],
                               tb_[:, 0:512].rearrange("p (a b) -> p a b", a=4), eng="act")
                    if pend_ep[0] is not None:
                        pend_ep[0]()
                    pend_ep[0] = epilogue
                if pend_ep[0] is not None:
                    pend_ep[0]()
                    pend_ep[0] = None
                if half == 0:
                    for j in range(2):
                        P.dma(st_d[(h * 2 + j) * 128:(h * 2 + j + 1) * 128, :], Sj[j])
            if l == 0:
                dump("glaoT%d" % half, RC, BF16)
            if stop_after == "d":
                return True

            tmpA = RDf[:, 0:4, :].rearrange("p (a b) c -> p a (b c)", b=2)
            tmpB = RDf[:, 4:8, :].rearrange("p (a b) c -> p a (b c)", b=2)
            for nb in range(8):
                def ev_ga(n, th, pa, msz):
                    P.activation(tmpA[:, n % 2, th * 512:(th + 1) * 512], pa, AF.Sigmoid)

                def ev_ya(n, th, pa, msz):
                    P.tt(tmpA[:, n % 2, th * 512:(th + 1) * 512], pa, tmpA[:, n % 2, th * 512:(th + 1) * 512], ALU.mult)

                def ev_gb(n, th, pa, msz):
                    P.activation(tmpB[:, n % 2, th * 512:(th + 1) * 512], pa, AF.Sigmoid)

                def ev_yb(n, th, pa, msz, nb=nb):
                    a = tmpA[:, n % 2, th * 512:(th + 1) * 512]
                    b_ = tmpB[:, n % 2, th * 512:(th + 1) * 512]
                    P.tt(b_, pa, b_, ALU.mult)
                    P.tt(a, a, b_, ALU.add)
                    mb = mixstg[:, (n * 2 + th) % 2, :]
                    P.copy(mb, a, eng="act")
                    P.dma(mixT_d[(nb * 2 + n % 2) * 128:(nb * 2 + n % 2 + 1) * 128, th * 512:(th + 1) * 512], mb)
                cs = slice(nb * 256, (nb + 1) * 256)
                linear_fm(w_in[l][:, C_GA + nb * 256:C_GA + (nb + 1) * 256], RA, KC, 256, ev_ga, blk=256)
                linear_fm(w_gla_o[l][:, cs], RC, KC, 256, ev_ya, blk=256)
                linear_fm(w_in[l][:, C_GB + nb * 256:C_GB + (nb + 1) * 256], RA, KC, 256, ev_gb, blk=256)
                linear_fm(w_sb_o[l][:, cs], RB, KC, 256, ev_yb, blk=256)
            if stop_after == "e":
                return True

            for kc in range(KC):
                P.dma(RA[:, kc, :], mixT_d[kc * 128:(kc + 1) * 128, :])
            if l == 0:
                dump("mixT%d" % half, RA[:], BF16)

            def ev_mo(n, th, pa, msz):
                P.copy(RBC[:, n, th * 512:(th + 1) * 512], pa, eng="dve")
                sumsq_accum(RBC[:, n, th * 512:(th + 1) * 512], n, th)
            linear_fm(w_out[l], RA, KC, D, ev_mo)
            resid_update(RBC, 32, xT_d)
            if stop_after == "f":
                return True

            norm_mod(48, 64, RA, xT_d)
            f1T = RD[:, 0:8, :]
            for fb in range(8):
                def ev_f1(n, th, pa, msz):
                    r_ = RDf[:, 8 + (n * 2 + th) % 2, :]
                    P.activation(r_, pa, AF.Relu)
                    P.tt(f1T[:, n % 8, th * 512:(th + 1) * 512], r_, r_, ALU.mult)
                linear_fm(w_ff1[l][:, fb * 1024:(fb + 1) * 1024], RA, KC, 1024, ev_f1)

                def ev_f2(n, th, pa, msz, fb=fb):
                    dst = RBC[:, n, th * 512:(th + 1) * 512]
                    if fb == 0:
                        P.copy(dst, pa, eng="act")
                    else:
                        P.tt(dst, dst, pa, ALU.add)
                    if fb == 7:
                        sumsq_accum(dst, n, th)
                linear_fm(w_ff2[l][fb * 1024:(fb + 1) * 1024, :], f1T, 8, D, ev_f2)
            resid_update(RBC, 80, xT_d)

            return False

        stop = False
        for l in range(depth):
            P.dma(gains[:], gainsT[l])
            P.dma(badT[:], b_adaT[l])
            P.dma(gn_bc[:], gn_bc_in[l])
            P.dma(wgu_sb[:], w_gu[l], q="pool")
            P.dma(bg_sb[:], b_gate[l], q="pool")
            pm = ps[3]
            for c0 in range(0, 6 * D, 512):
                wb = wload(w_ada[l][:, c0:c0 + 512], KC, 512)
                for j in range(4):
                    col = c0 // 128 + j
                    for kc in range(KC):
                        P.matmul(pm[:, col:col + 1], wb[:, kc, j * 128:(j + 1) * 128], cact[:, kc:kc + 1],
                                 start=(kc == 0), stop=(kc == KC - 1))
            P.tt(modT[:], pm[:, 0:96], badT[:], ALU.add)
            P.stt(coef[:, 0:16], modT[:, 16:32], 1.0, gains[:, 0:16], ALU.add, ALU.mult)
            P.copy(coef[:, 16:32], modT[:, 0:16])
            P.tt(coef[:, 32:48], modT[:, 32:48], gains[:, 16:32], ALU.mult)
            P.stt(coef[:, 48:64], modT[:, 64:80], 1.0, gains[:, 32:48], ALU.add, ALU.mult)
            P.copy(coef[:, 64:80], modT[:, 48:64])
            P.tt(coef[:, 80:96], modT[:, 80:96], gains[:, 48:64], ALU.mult)
            if l == 0:
                dump("modT", modT[:])

            for half in range(2):
                if layer_half(l, half):
                    stop = True
                    break
            if stop:
                break

        for tt16 in range(16):
            xT_d = xT_dd[tt16 // 8]
            tt = tt16 % 8
            for half in range(2):
                o = rdf((tt * 2 + half) % 2)
                for g in range(2):
                    kc0 = half * 8 + g * 4
                    src = rdf(2 + g)
                    P.dma(src[:, 0:512].rearrange("p (j t) -> p j t", j=4),
                          xT_d[kc0 * 128:(kc0 + 4) * 128, tt * 128:(tt + 1) * 128].rearrange("(j p) t -> p j t", p=128))
                    bank = next_bank()
                    for j in range(4):
                        P.transpose(bank[:, j * 128:(j + 1) * 128], src[:, j * 128:(j + 1) * 128], ident_f)
                    P.copy(o[:, g * 512:(g + 1) * 512], bank[:], eng=("act" if g else "dve"))
                P.dma(y_out[tt16 * 128:(tt16 + 1) * 128, half * 1024:(half + 1) * 1024], o, final=True)
        P.emit()
    return nc


def make_consts():
    c = np.zeros((128, 1024), np.float32)
    p = np.arange(128)[:, None]
    f = np.arange(128)[None, :]
    c[:, 0:128] = (p == f)
    c[:, 128:256] = (f < p)
    c[:, 256] = (np.arange(128) < 64)
    c[:, 257] = (np.arange(128) >= 64)
    c[:, 384:512] = 1.0 / 2048.0
    same = (p // 64) == (f // 64)
    c[:, 512:640] = np.where((p > f) & same, -1.0 / 16.0, 0.0)
    c[:, 640] = np.where(np.arange(128) < 64, -1.0 / 16.0, 0.0)
    c[:, 641] = np.where(np.arange(128) >= 64, -1.0 / 16.0, 0.0)
    c[:, 768:896] = 1.0
    return c


def make_in_maps(inputs, n_cores=8, depth=DEPTH):
    f32 = lambda a: np.ascontiguousarray(np.asarray(a, dtype=np.float32))
    x = f32(inputs["x"])
    c = f32(inputs["c"])
    L = DEPTH
    shared0 = {
        "w_ada": f32(inputs["w_ada"]),
        "w_in": f32(inputs["w_in"]),
        "w_gate_up": f32(inputs["w_gate_up"]),
        "w_gla_o": f32(inputs["w_gla_o"]),
        "w_sb_o": f32(inputs["w_sb_o"]),
        "w_out": f32(inputs["w_out"]),
        "w_ff1": f32(inputs["w_ff1"]),
        "w_ff2": f32(inputs["w_ff2"]),
        "b_adaT": np.ascontiguousarray(f32(inputs["b_ada"]).reshape(L, 96, 128).transpose(0, 2, 1)),
        "gainsT": np.ascontiguousarray(f32(inputs["norm_gains"]).reshape(L, 4, KC, 128).transpose(0, 3, 1, 2).reshape(L, 128, 4 * KC)),
        "b_gate": f32(inputs["b_gate"]).reshape(L, 1, 1024),
        "gn_bc": np.ascontiguousarray(np.broadcast_to(f32(inputs["gla_norm_gain"]).reshape(L, 1, 2048), (L, 128, 2048))),
    }
    shared = {k: np.ascontiguousarray(v[:depth]) for k, v in shared0.items()}
    shared["consts"] = make_consts()
    maps = []
    for core in range(n_cores):
        b = core % 4
        m = dict(shared)
        m["x"] = np.ascontiguousarray(x[b])
        m["cT"] = np.ascontiguousarray(c[b].reshape(KC, 128).T)
        maps.append(m)
    return maps


_NC_CACHE = {}
N_LAUNCH_CORES = 8


def kernel(**inputs):
    n = N_LAUNCH_CORES
    if "nc" not in _NC_CACHE:
        _NC_CACHE["nc"] = build_program(n)
    nc = _NC_CACHE["nc"]
    maps = make_in_maps(inputs, n)
    res = run_bass_kernel_spmd(nc, maps, core_ids=list(range(n)))
    out = np.empty((4, 2 * T, D), np.float32)
    for b in range(4):
        out[b] = np.asarray(res.results[b + 4]["y"], dtype=np.float32)
    if os.environ.get("KDIAG"):
        for b in range(4):
            a0 = np.asarray(res.results[b]["y"], dtype=np.float32)
            print("[kdiag] batch", b, "core", b, "vs core", b + 4, "maxabs diff", float(np.abs(a0 - out[b]).max()), flush=True)
    return out
```
